# Optimizing a Trainium2 kernel written in Bass

```python
import math
import jax, jax.numpy as jnp
from jax import lax
import numpy as np

D_MODEL = 2048
BATCH = 2
SEQ = 16384
DEPTH = 4

GRID_W = 64
CTX_LEN = 256

LRU_WIDTH = D_MODEL
LRU_BLOCKS = 16
LRU_BLOCK = LRU_WIDTH // LRU_BLOCKS
LRU_CONV = 4
LRU_C = 8.0
SC_WIDTH = D_MODEL
SC_CONV = 3
HEAD_DIM = 128
N_Q_HEADS = 16
N_KV_HEADS = 4
Q_WIDTH = N_Q_HEADS * HEAD_DIM
KV_WIDTH = N_KV_HEADS * HEAD_DIM
Q_BLOCK = 128
ROPE_THETA = 10000.0
SSM_WIDTH = D_MODEL
SSM_HEAD_DIM = 64
SSM_HEADS = SSM_WIDTH // SSM_HEAD_DIM
SSM_GROUPS = 4
SSM_STATE = 128
SSM_CONV = 4
SSM_CHUNK = 128
SSM_CONV_DIM = SSM_WIDTH + 2 * SSM_GROUPS * SSM_STATE
N_BRANCH = 4
D_FF = 5632
FFN_CONV = 3
DEEPNORM_ALPHA = (2 * DEPTH) ** 0.25
DEEPNORM_BETA = (8 * DEPTH) ** -0.25
LN_EPS = 1e-5
RMS_EPS = 1e-6

IN_WIDTHS = (LRU_WIDTH, LRU_WIDTH,
             SC_WIDTH, SC_WIDTH, SC_WIDTH,
             Q_WIDTH, KV_WIDTH, KV_WIDTH,
             SSM_WIDTH, SSM_CONV_DIM, 2 * SSM_HEADS,
             N_BRANCH * D_MODEL)
IN_WIDTH = (2 * LRU_WIDTH + 3 * SC_WIDTH + Q_WIDTH + 2 * KV_WIDTH
            + SSM_WIDTH + SSM_CONV_DIM + 2 * SSM_HEADS + N_BRANCH * D_MODEL)

kernel_name = 'hybrid_parallel_gated_flow_backbone'


def dwconv(x, w, pad):
    return lax.conv_general_dilated(
        x, w[:, None, :].astype(x.dtype), window_strides=(1,), padding=[pad],
        dimension_numbers=('NWC', 'WIO', 'NWC'), feature_group_count=x.shape[-1])


def layer_norm(x, g, b):
    xf = x.astype(jnp.float32)
    mu = jnp.mean(xf, axis=-1, keepdims=True)
    var = jnp.mean(jnp.square(xf - mu), axis=-1, keepdims=True)
    y = (xf - mu) * lax.rsqrt(var + LN_EPS) * g.astype(jnp.float32) + b.astype(jnp.float32)
    return y.astype(x.dtype)


def rms_norm(x, g):
    xf = x.astype(jnp.float32)
    y = xf * lax.rsqrt(jnp.mean(jnp.square(xf), axis=-1, keepdims=True) + RMS_EPS)
    return (y * g.astype(jnp.float32)).astype(x.dtype)


def axial_rope_tables(rows):
    r = jnp.broadcast_to(jnp.arange(rows, dtype=jnp.float32)[:, None], (rows, GRID_W)).reshape(-1)
    cidx = jnp.broadcast_to(jnp.arange(GRID_W, dtype=jnp.float32)[None, :], (rows, GRID_W)).reshape(-1)
    n_freq = HEAD_DIM // 4
    inv = ROPE_THETA ** (-jnp.arange(n_freq, dtype=jnp.float32) / n_freq)
    ang_r = (r[:, None] * inv)[:, None, :]
    ang_c = (cidx[:, None] * inv)[:, None, :]
    return (jnp.cos(ang_r), jnp.sin(ang_r), jnp.cos(ang_c), jnp.sin(ang_c))


def _rotate_half(v, cos, sin):
    v1, v2 = jnp.split(v, 2, axis=-1)
    return jnp.concatenate([v1 * cos - v2 * sin, v2 * cos + v1 * sin], axis=-1)


def apply_axial_rope(x, rope):
    cos_r, sin_r, cos_c, sin_c = rope
    xr, xc = jnp.split(x, 2, axis=-1)
    y = jnp.concatenate([_rotate_half(xr, cos_r, sin_r), _rotate_half(xc, cos_c, sin_c)], axis=-1)
    return y.astype(x.dtype)


def attend(q, k, v):
    s = jnp.einsum('bqkgd,bskd->bkgqs', q, k, preferred_element_type=jnp.float32) * (HEAD_DIM ** -0.5)
    p = jax.nn.softmax(s, axis=-1)
    return jnp.einsum('bkgqs,bskd->bqkgd', p.astype(v.dtype), v)


def attention_mixer(q_c, k_c, v_c, q_l, k_l, v_l, q_norm, k_norm, rope, with_ctx):
    bsz, n_ctx = q_c.shape[:2]
    n_lat = q_l.shape[1]
    grp = N_Q_HEADS // N_KV_HEADS
    q_c = rms_norm(q_c.reshape(bsz, n_ctx, N_Q_HEADS, HEAD_DIM), q_norm)
    k_c = rms_norm(k_c.reshape(bsz, n_ctx, N_KV_HEADS, HEAD_DIM), k_norm)
    v_c = v_c.reshape(bsz, n_ctx, N_KV_HEADS, HEAD_DIM)
    q_l = apply_axial_rope(rms_norm(q_l.reshape(bsz, n_lat, N_Q_HEADS, HEAD_DIM), q_norm), rope)
    k_l = apply_axial_rope(rms_norm(k_l.reshape(bsz, n_lat, N_KV_HEADS, HEAD_DIM), k_norm), rope)
    v_l = v_l.reshape(bsz, n_lat, N_KV_HEADS, HEAD_DIM)
    y_c = None
    if with_ctx:
        y_c = attend(q_c.reshape(bsz, n_ctx, N_KV_HEADS, grp, HEAD_DIM), k_c, v_c).reshape(bsz, n_ctx, Q_WIDTH)
    k_all = jnp.concatenate([k_c, k_l], axis=1)
    v_all = jnp.concatenate([v_c, v_l], axis=1)
    n_blk = n_lat // Q_BLOCK
    q_blk = jnp.moveaxis(q_l.reshape(bsz, n_blk, Q_BLOCK, N_KV_HEADS, grp, HEAD_DIM), 1, 0)
    y_blk = lax.map(lambda qb: attend(qb, k_all, v_all), q_blk)
    y_l = jnp.moveaxis(y_blk, 0, 1).reshape(bsz, n_lat, Q_WIDTH)
    return y_c, y_l


def _affine_combine(left, right):
    a_l, b_l = left
    a_r, b_r = right
    return a_r * a_l, a_r * b_l + b_r


def linear_scan(a, b, h0, reverse):
    if reverse:
        a, b = jnp.flip(a, 1), jnp.flip(b, 1)
    a_cum, b_cum = lax.associative_scan(_affine_combine, (a, b), axis=1)
    h = a_cum * h0[:, None] + b_cum
    final = h[:, -1]
    if reverse:
        h = jnp.flip(h, 1)
    return h, final


def rglru_coeffs(u, w_r, b_r, w_i, b_i, lam):
    uh = u.reshape(u.shape[0], u.shape[1], LRU_BLOCKS, LRU_BLOCK)
    r = jax.nn.sigmoid(jnp.einsum('bshi,hij->bshj', uh, w_r.astype(jnp.float32)).reshape(u.shape)
                       + b_r.astype(jnp.float32))
    i = jax.nn.sigmoid(jnp.einsum('bshi,hij->bshj', uh, w_i.astype(jnp.float32)).reshape(u.shape)
                       + b_i.astype(jnp.float32))
    log_a = -LRU_C * r * jax.nn.softplus(-lam.astype(jnp.float32))
    a = jnp.exp(log_a)
    b = jnp.sqrt(-jnp.expm1(2.0 * log_a)) * (i * u)
    return a, b


def rglru_mixer(x_c, g_c, x_l, g_l, conv_w, conv_b, gate_w, gate_b, lam):
    pad = (LRU_CONV // 2, LRU_CONV - 1 - LRU_CONV // 2)
    u_c = (dwconv(x_c, conv_w, pad) + conv_b).astype(jnp.float32)
    u_l = (dwconv(x_l, conv_w, pad) + conv_b).astype(jnp.float32)
    h0 = jnp.zeros((x_c.shape[0], LRU_WIDTH), jnp.float32)
    y_c = jnp.zeros_like(u_c)
    y_l = jnp.zeros_like(u_l)
    for d in range(2):
        rev = d == 1
        prm = (gate_w[d, 0], gate_b[d, 0], gate_w[d, 1], gate_b[d, 1], lam[d])
        h_c, s_c = linear_scan(*rglru_coeffs(u_c, *prm), h0, rev)
        h_l, _ = linear_scan(*rglru_coeffs(u_l, *prm), s_c, rev)
        y_c = y_c + h_c
        y_l = y_l + h_l
    return (jax.nn.gelu(g_c) * y_c.astype(g_c.dtype), jax.nn.gelu(g_l) * y_l.astype(g_l.dtype))


def short_conv_mixer(b_gate, c_gate, xs, conv_w):
    pad = (SC_CONV // 2, SC_CONV - 1 - SC_CONV // 2)
    return b_gate * dwconv(c_gate * xs, conv_w, pad)


def ssd_chunked(x, dt, a_neg, bm, cm, h0):
    b, l, h, p = x.shape
    g, n = bm.shape[2], bm.shape[3]
    k = h // g
    q = SSM_CHUNK
    c = l // q
    xx = (x * dt[..., None]).reshape(b, c, q, g, k, p)
    a = jnp.moveaxis((dt * a_neg).reshape(b, c, q, g, k), 2, -1)
    a_cs = jnp.cumsum(a, axis=-1)
    bc = bm.reshape(b, c, q, g, n)
    cc = cm.reshape(b, c, q, g, n)
    causal = jnp.tril(jnp.ones((q, q), dtype=bool))
    seg = jnp.exp(jnp.where(causal, a_cs[..., :, None] - a_cs[..., None, :], -jnp.inf))
    cb = jnp.einsum('bclgn,bcsgn->bcgls', cc, bc)
    y_diag = jnp.einsum('bcgls,bcgkls,bcsgkp->bclgkp', cb, seg, xx)
    decay = jnp.exp(a_cs[..., -1:] - a_cs)
    states = jnp.einsum('bcsgn,bcgks,bcsgkp->bcgkpn', bc, decay, xx)
    chunk_decay = jnp.exp(a_cs[..., -1])

    def step(h_prev, inp):
        dec, st = inp
        return dec[..., None, None] * h_prev + st, h_prev

    h_last, h_in = lax.scan(step, h0.reshape(b, g, k, p, n),
                            (jnp.moveaxis(chunk_decay, 1, 0), jnp.moveaxis(states, 1, 0)))
    h_in = jnp.moveaxis(h_in, 0, 1)
    y_off = jnp.einsum('bclgn,bcgkpn,bcgkl->bclgkp', cc, h_in, jnp.exp(a_cs))
    y = (y_diag + y_off).reshape(b, l, h, p)
    return y, h_last.reshape(b, h, p, n)


def ssd_direction(x, dt, a_neg, bm, cm, h0, reverse):
    if reverse:
        x, dt, bm, cm = (jnp.flip(t, 1) for t in (x, dt, bm, cm))
    y, final = ssd_chunked(x, dt, a_neg, bm, cm, h0)
    if reverse:
        y = jnp.flip(y, 1)
    return y, final


def ssd_mixer(z_c, xbc_c, dt_c, z_l, xbc_l, dt_l, conv_w, conv_b, a_log, dt_bias, d_skip, norm_w):
    pad = (SSM_CONV // 2, SSM_CONV - 1 - SSM_CONV // 2)
    cut = [SSM_WIDTH, SSM_WIDTH + SSM_GROUPS * SSM_STATE]

    def prep(xbc, dt_raw):
        u = jax.nn.silu(dwconv(xbc, conv_w, pad) + conv_b).astype(jnp.float32)
        bsz, sl = u.shape[:2]
        xs, bm, cm = jnp.split(u, cut, axis=-1)
        return (xs.reshape(bsz, sl, SSM_HEADS, SSM_HEAD_DIM),
                bm.reshape(bsz, sl, SSM_GROUPS, SSM_STATE),
                cm.reshape(bsz, sl, SSM_GROUPS, SSM_STATE),
                dt_raw.astype(jnp.float32).reshape(bsz, sl, 2, SSM_HEADS))

    xc, bc, cc, dtc = prep(xbc_c, dt_c)
    xl, bl, cl, dtl = prep(xbc_l, dt_l)
    dsk = d_skip.astype(jnp.float32)[:, None]
    y_c = dsk * xc
    y_l = dsk * xl
    h0 = jnp.zeros((xc.shape[0], SSM_HEADS, SSM_HEAD_DIM, SSM_STATE), jnp.float32)
    for d in range(2):
        rev = d == 1
        a_neg = -jnp.exp(a_log[d].astype(jnp.float32))
        bias = dt_bias[d].astype(jnp.float32)
        yc_d, s_c = ssd_direction(xc, jax.nn.softplus(dtc[:, :, d] + bias), a_neg, bc, cc, h0, rev)
        yl_d, _ = ssd_direction(xl, jax.nn.softplus(dtl[:, :, d] + bias), a_neg, bl, cl, s_c, rev)
        y_c = y_c + yc_d
        y_l = y_l + yl_d

    def gated_norm(y, z):
        bsz, sl = z.shape[:2]
        yg = (y.reshape(bsz, sl, SSM_WIDTH) * jax.nn.silu(z.astype(jnp.float32)))
        yg = yg.reshape(bsz, sl, SSM_GROUPS, SSM_WIDTH // SSM_GROUPS)
        yg = yg * lax.rsqrt(jnp.mean(jnp.square(yg), axis=-1, keepdims=True) + RMS_EPS)
        return (yg.reshape(bsz, sl, SSM_WIDTH) * norm_w.astype(jnp.float32)).astype(z.dtype)

    return gated_norm(y_c, z_c), gated_norm(y_l, z_l)


def merge_branches(branches, w_branches, gates, w_out):
    g = jax.nn.sigmoid(gates.reshape(gates.shape[:-1] + (N_BRANCH, D_MODEL)))
    m = g[..., 0, :] * (branches[0] @ w_branches[0])
    for j in range(1, N_BRANCH):
        m = m + g[..., j, :] * (branches[j] @ w_branches[j])
    return m @ w_out


def token_mixing(h_c, h_l, w_in, lru_conv_w, lru_conv_b, lru_gate_w, lru_gate_b, lru_lambda,
                 sconv_w, q_norm, k_norm, ssm_conv_w, ssm_conv_b, ssm_a_log, ssm_dt_bias, ssm_d,
                 ssm_norm, w_br_lru, w_br_sconv, w_br_attn, w_br_ssm, w_out, rope, with_ctx):
    cut = np.cumsum(IN_WIDTHS)[:-1].tolist()
    (ax_c, ag_c, sb_c, sg_c, sx_c, q_c, k_c, v_c, z_c, xbc_c, dt_c, mg_c) = jnp.split(h_c @ w_in, cut, axis=-1)
    (ax_l, ag_l, sb_l, sg_l, sx_l, q_l, k_l, v_l, z_l, xbc_l, dt_l, mg_l) = jnp.split(h_l @ w_in, cut, axis=-1)
    lru_c, lru_l = rglru_mixer(ax_c, ag_c, ax_l, ag_l, lru_conv_w, lru_conv_b, lru_gate_w, lru_gate_b, lru_lambda)
    att_c, att_l = attention_mixer(q_c, k_c, v_c, q_l, k_l, v_l, q_norm, k_norm, rope, with_ctx)
    ssm_c, ssm_l = ssd_mixer(z_c, xbc_c, dt_c, z_l, xbc_l, dt_l, ssm_conv_w, ssm_conv_b,
                             ssm_a_log, ssm_dt_bias, ssm_d, ssm_norm)
    w_br = (w_br_lru, w_br_sconv, w_br_attn, w_br_ssm)
    m_l = merge_branches((lru_l, short_conv_mixer(sb_l, sg_l, sx_l, sconv_w), att_l, ssm_l), w_br, mg_l, w_out)
    m_c = None
    if with_ctx:
        m_c = merge_branches((lru_c, short_conv_mixer(sb_c, sg_c, sx_c, sconv_w), att_c, ssm_c), w_br, mg_c, w_out)
    return m_c, m_l


def conv_ffn(h, w_up, conv_w, conv_b, w_down):
    pad = (FFN_CONV // 2, FFN_CONV - 1 - FFN_CONV // 2)
    u = dwconv(h @ w_up, conv_w, pad) + conv_b
    g, v = jnp.split(u, 2, axis=-1)
    return (jax.nn.silu(g) * v) @ w_down


def setup_inputs(seed: int = 0) -> dict:
    key = jax.random.key(seed)
    ks = jax.random.split(key, 40)
    f32 = jnp.float32
    L = DEPTH

    def nrm(k, shape, scale):
        return jax.random.normal(k, shape, f32) * scale

    s_lam = jax.random.uniform(ks[10], (L, 2, LRU_WIDTH), f32, 0.9, 0.999) ** (1.0 / LRU_C)
    lru_lambda = jnp.log(s_lam) - jnp.log1p(-s_lam)
    ssm_a_log = jnp.log(jax.random.uniform(ks[16], (L, 2, SSM_HEADS), f32, 1.0, 16.0))
    dt0 = jnp.exp(jax.random.uniform(ks[17], (L, 2, SSM_HEADS), f32, math.log(1e-3), math.log(1e-1)))
    ssm_dt_bias = dt0 + jnp.log(-jnp.expm1(-dt0))
    return {
        'x': nrm(ks[0], (BATCH, SEQ, D_MODEL), 1.0),
        'c': nrm(ks[1], (BATCH, D_MODEL), 1.0),
        'ctx': nrm(ks[2], (BATCH, CTX_LEN, D_MODEL), 1.0),
        'c_ctx': nrm(ks[3], (D_MODEL,), 1.0),
        'w_ada': nrm(ks[4], (L, D_MODEL, 6 * D_MODEL), 0.5 * D_MODEL ** -0.5),
        'w_in': nrm(ks[5], (L, D_MODEL, IN_WIDTH), D_MODEL ** -0.5),
        'lru_conv_w': nrm(ks[6], (L, LRU_CONV, LRU_WIDTH), LRU_CONV ** -0.5),
        'lru_conv_b': nrm(ks[7], (L, LRU_WIDTH), 0.02),
        'lru_gate_w': nrm(ks[8], (L, 2, 2, LRU_BLOCKS, LRU_BLOCK, LRU_BLOCK), LRU_BLOCK ** -0.5),
        'lru_gate_b': nrm(ks[9], (L, 2, 2, LRU_WIDTH), 0.02),
        'lru_lambda': lru_lambda,
        'sconv_w': nrm(ks[11], (L, SC_CONV, SC_WIDTH), SC_CONV ** -0.5),
        'attn_q_norm': 1.0 + nrm(ks[12], (L, HEAD_DIM), 0.1),
        'attn_k_norm': 1.0 + nrm(ks[13], (L, HEAD_DIM), 0.1),
        'ssm_conv_w': nrm(ks[14], (L, SSM_CONV, SSM_CONV_DIM), SSM_CONV ** -0.5),
        'ssm_conv_b': nrm(ks[15], (L, SSM_CONV_DIM), 0.02),
        'ssm_a_log': ssm_a_log,
        'ssm_dt_bias': ssm_dt_bias,
        'ssm_d': 1.0 + nrm(ks[18], (L, SSM_HEADS), 0.1),
        'ssm_norm': 1.0 + nrm(ks[19], (L, SSM_WIDTH), 0.1),
        'w_br_lru': nrm(ks[20], (L, LRU_WIDTH, D_MODEL), LRU_WIDTH ** -0.5),
        'w_br_sconv': nrm(ks[21], (L, SC_WIDTH, D_MODEL), SC_WIDTH ** -0.5),
        'w_br_attn': nrm(ks[22], (L, Q_WIDTH, D_MODEL), Q_WIDTH ** -0.5),
        'w_br_ssm': nrm(ks[23], (L, SSM_WIDTH, D_MODEL), SSM_WIDTH ** -0.5),
        'w_out': nrm(ks[24], (L, D_MODEL, D_MODEL), DEEPNORM_BETA * D_MODEL ** -0.5),
        'ln_g': 1.0 + nrm(ks[25], (L, 2, D_MODEL), 0.1),
        'ln_b': nrm(ks[26], (L, 2, D_MODEL), 0.02),
        'ffn_up': nrm(ks[27], (L, D_MODEL, 2 * D_FF), D_MODEL ** -0.5),
        'ffn_conv_w': nrm(ks[28], (L, FFN_CONV, 2 * D_FF), FFN_CONV ** -0.5),
        'ffn_conv_b': nrm(ks[29], (L, 2 * D_FF), 0.02),
        'ffn_down': nrm(ks[30], (L, D_FF, D_MODEL), DEEPNORM_BETA * D_FF ** -0.5),
    }


def reference(x, c, ctx, c_ctx, w_ada, w_in, lru_conv_w, lru_conv_b, lru_gate_w, lru_gate_b,
              lru_lambda, sconv_w, attn_q_norm, attn_k_norm, ssm_conv_w, ssm_conv_b, ssm_a_log,
              ssm_dt_bias, ssm_d, ssm_norm, w_br_lru, w_br_sconv, w_br_attn, w_br_ssm, w_out,
              ln_g, ln_b, ffn_up, ffn_conv_w, ffn_conv_b, ffn_down):
    rows = x.shape[1] // GRID_W
    rope = axial_rope_tables(rows)
    s_lat = jax.nn.silu(c)[:, None, :]
    s_ctx = jax.nn.silu(c_ctx)
    h = x
    h_ctx = ctx
    for i in range(DEPTH):
        with_ctx = i < DEPTH - 1
        sh1, sc1, g1, sh2, sc2, g2 = jnp.split(s_lat @ w_ada[i], 6, axis=-1)
        csh1, csc1, cg1, csh2, csc2, cg2 = jnp.split(s_ctx @ w_ada[i], 6, axis=-1)
        m_c, m_l = token_mixing(
            h_ctx * (1.0 + csc1) + csh1, h * (1.0 + sc1) + sh1, w_in[i],
            lru_conv_w[i], lru_conv_b[i], lru_gate_w[i], lru_gate_b[i], lru_lambda[i],
            sconv_w[i], attn_q_norm[i], attn_k_norm[i], ssm_conv_w[i], ssm_conv_b[i],
            ssm_a_log[i], ssm_dt_bias[i], ssm_d[i], ssm_norm[i],
            w_br_lru[i], w_br_sconv[i], w_br_attn[i], w_br_ssm[i], w_out[i], rope, with_ctx)
        h = layer_norm(DEEPNORM_ALPHA * h + g1 * m_l, ln_g[i, 0], ln_b[i, 0])
        f_l = conv_ffn(h * (1.0 + sc2) + sh2, ffn_up[i], ffn_conv_w[i], ffn_conv_b[i], ffn_down[i])
        h = layer_norm(DEEPNORM_ALPHA * h + g2 * f_l, ln_g[i, 1], ln_b[i, 1])
        if with_ctx:
            h_ctx = layer_norm(DEEPNORM_ALPHA * h_ctx + cg1 * m_c, ln_g[i, 0], ln_b[i, 0])
            f_c = conv_ffn(h_ctx * (1.0 + csc2) + csh2, ffn_up[i], ffn_conv_w[i], ffn_conv_b[i], ffn_down[i])
            h_ctx = layer_norm(DEEPNORM_ALPHA * h_ctx + cg2 * f_c, ln_g[i, 1], ln_b[i, 1])
    return h
```

```python
import contextlib
import re
import math
import numpy as np
import concourse.bass as bass
import concourse.mybir as mybir
from concourse.bass_utils import run_bass_kernel_spmd

F32 = mybir.dt.float32
BF16 = mybir.dt.bfloat16
AF = mybir.ActivationFunctionType
ALU = mybir.AluOpType

D = 2048
NCH = 16
CTX = 256
IN_W = 26688
DFF = 5632
ALPHA = 8.0 ** 0.25
LN_EPS = 1e-5
RMS_EPS = 1e-6
O_AX, O_AG, O_SB, O_SG, O_SX, O_Q, O_K, O_V, O_Z, O_XBC, O_MG, O_DT = (
    0, 2048, 4096, 6144, 8192, 10240, 12288, 12800, 13312, 15360, 18432, 26624)
W_IN_DT0, W_IN_MG0 = 18432, 18496


class Split:
    def __init__(self, pieces):
        self.pieces = pieces

    def __getitem__(self, key):
        rs, cs = key
        for (a, b, ap) in self.pieces:
            if a <= rs.start and rs.stop <= b:
                return ap[rs.start - a:rs.stop - a, cs]
        raise ValueError("row range %s crosses scratch pieces" % (rs,))


class Stack3:
    def __init__(self, aps):
        self.aps = aps

    def __getitem__(self, key):
        j, rs, cs = key
        return self.aps[j][rs, cs]


_UID = [0]


def U(name):
    _UID[0] += 1
    return "%s_%d" % (name, _UID[0])


class Buf:
    __slots__ = ("name",)

    def __init__(self, name):
        self.name = name


def dsl(start, size):
    if isinstance(start, int):
        return slice(start, start + size)
    return bass.ds(start, size)


class Prog:
    NS = 12

    def __init__(self, nc, stack):
        self.nc = nc
        self.E = {"pe": nc.tensor, "act": nc.scalar, "dve": nc.vector, "pool": nc.gpsimd, "sp": nc.sync}
        self.sem = {}
        self.base = {}
        for e in self.E:
            self.sem[e] = stack.enter_context(nc.semaphore("s_" + e))
            self.base[e] = 0
        self.dsems = {}
        for q in ("sp", "pool", "act"):
            self.dsems[q] = []
            for k in range(self.NS):
                nm = "d_%s%d" % (q, k)
                self.sem[nm] = stack.enter_context(nc.semaphore(nm))
                self.base[nm] = 0
                self.dsems[q].append(nm)
        self.rt = {e: self.E[e].alloc_register("rt_" + e) for e in self.E}
        self.rt2 = {e: self.E[e].alloc_register("rtd_" + e) for e in self.E}
        self.loopregs = nc.alloc_registers("loop_i", engines=mybir.ALL_ENGINES)
        self.ET = {"pe": mybir.EngineType.PE, "act": mybir.EngineType.Activation, "dve": mybir.EngineType.DVE,
                   "pool": mybir.EngineType.Pool, "sp": mybir.EngineType.SP}
        self.sem["cc"] = stack.enter_context(nc.semaphore("s_cc"))
        self.base["cc"] = 0
        self.rt3 = {e: self.E[e].alloc_register("rtc_" + e) for e in ("sp", "pool", "act")}
        self.rcore = {e: self.E[e].alloc_register("rcore_" + e) for e in ("sp", "pool", "act")}
        self.ops = []
        self.nbuf = 0

    def buf(self, name=None):
        self.nbuf += 1
        return Buf(name or "b%d" % self.nbuf)

    def op(self, eng, fn, r=(), w=()):
        self.ops.append((eng, fn, tuple(r), tuple(w), False))

    def dma(self, q, fn, r=(), w=()):
        self.ops.append((q, fn, tuple(r), tuple(w), True))

    def load_core_id(self, cid_ap):
        for e in ("sp", "pool", "act"):
            self.E[e].reg_load(self.rcore[e], cid_ap)

    def coll(self, in_ap, out_ap, groups, r=(), w=()):
        def fn(idx):
            return self.nc.gpsimd.collective_compute("AllGather", mybir.AluOpType.bypass, replica_groups=groups,
                                                     ins=[in_ap.opt()], outs=[out_ap.opt()])
        self.ops.append(("pool", fn, tuple(r), tuple(w), 2))

    def dma2(self, q, out_fn, in_fn, r=(), w=(), cj=(0, 0)):
        def fn(idx):
            e = self.E[q]
            if isinstance(idx, int) and cj == (0, 0):
                return e.dma_start(out=out_fn(idx), in_=in_fn(idx))
            aps = []
            for f, c_ in zip((out_fn, in_fn), cj):
                if isinstance(idx, int):
                    a0 = f(idx)
                    k = 0
                else:
                    a0, a1 = f(0), f(1)
                    k = a1.offset - a0.offset
                if k == 0 and c_ == 0:
                    aps.append(a0)
                    continue
                rt = self.rt2[q]
                if k != 0:
                    e.reg_mul(rt, idx[self.ET[q]], k)
                    e.reg_add(rt, rt, a0.offset)
                else:
                    e.reg_mov(rt, a0.offset)
                if c_ != 0:
                    e.reg_mul(self.rt3[q], self.rcore[q], c_)
                    e.reg_add(rt, rt, self.rt3[q])
                aps.append(bass.AP(a0.tensor, rt, [list(x) for x in a0.ap]))
            ins = e.dma_start(out=aps[0], in_=aps[1])
            m = re.search(r"R\[(\w+?)_tmp_(\d+)\]", ins.concise())
            if m:
                pre, tid = m.group(1), int(m.group(2))
                RH = type(self.rt2[q])
                for nm in ("%s_tmp_%d" % (pre, tid), "%s_%s_rtd_%s_snap_%d" % (pre, pre, q, tid - 2)):
                    e.free_register(RH(nm, self.rt2[q].engine))
            return ins
        self.ops.append((q, fn, tuple(r), tuple(w), True))

    def emit(self, T=1):
        ops = self.ops
        self.ops = []
        n = len(ops)
        lastw = {}
        readers = {}
        deps = [None] * n
        for j, (eng, fn, r, w, isd) in enumerate(ops):
            dj = set()
            for b in r:
                if b in lastw:
                    dj.add(lastw[b])
            for b in w:
                if b in lastw:
                    dj.add(lastw[b])
                for k in readers.get(b, ()):
                    dj.add(k)
            dj.discard(j)
            deps[j] = dj
            for b in r:
                readers.setdefault(b, []).append(j)
            for b in w:
                lastw[b] = j
                readers[b] = []
        signal = [False] * n
        for j in range(n):
            for d in deps[j]:
                if ops[d][0] == "pe" and ops[j][0] == "pe" and not ops[d][4] and not ops[j][4]:
                    continue
                signal[d] = True
        lastop = {}
        for j in range(n):
            if ops[j][4]:
                signal[j] = True
            else:
                lastop[ops[j][0]] = j
        for e, j in lastop.items():
            signal[j] = True
        cnt = {}
        sig = [None] * n
        slot_prev = {}
        pre_wait = [None] * n
        rr = {"sp": 0, "pool": 0, "act": 0}
        for j in range(n):
            eng, fn, r, w, isd = ops[j]
            if not signal[j]:
                continue
            if isd == 2:
                if "cc" in slot_prev:
                    pre_wait[j] = slot_prev["cc"]
                cnt["cc"] = cnt.get("cc", 0) + 1
                sig[j] = ("cc", cnt["cc"])
                slot_prev["cc"] = sig[j]
            elif isd:
                s = self.dsems[eng][rr[eng] % self.NS]
                rr[eng] += 1
                if s in slot_prev:
                    pre_wait[j] = slot_prev[s]
                cnt[s] = cnt.get(s, 0) + 16
                sig[j] = (s, cnt[s])
                slot_prev[s] = sig[j]
            else:
                cnt[eng] = cnt.get(eng, 0) + 1
                sig[j] = (eng, cnt[eng])
        nc = self.nc

        def body(idx):
            waited = {e: {} for e in self.E}

            def wait(e, s, c):
                if waited[e].get(s, 0) >= c:
                    return
                waited[e][s] = c
                if isinstance(idx, int):
                    self.E[e].wait_ge(self.sem[s], self.base[s] + idx * cnt[s] + c)
                else:
                    rt = self.rt[e]
                    self.E[e].reg_mul(rt, idx[self.ET[e]], cnt[s])
                    self.E[e].reg_add(rt, rt, self.base[s] + c)
                    self.E[e].wait_ge(self.sem[s], rt)

            for j in range(n):
                eng, fn, r, w, isd = ops[j]
                for d in sorted(deps[j]):
                    if sig[d] is None:
                        continue
                    if (not isd) and eng == "pe" and ops[d][0] == "pe" and not ops[d][4]:
                        continue
                    wait(eng, *sig[d])
                if pre_wait[j] is not None:
                    wait(eng, *pre_wait[j])
                ins = fn(idx)
                if sig[j] is not None:
                    if isd == 2:
                        ins.then_inc(self.sem["cc"])
                    else:
                        ins.then_inc(self.sem[sig[j][0]], 16 if isd else 1)
            for q in ("sp", "pool", "act"):
                for s in self.dsems[q]:
                    if s in cnt:
                        wait(q, s, cnt[s])
            if "cc" in cnt:
                wait("pool", "cc", cnt["cc"])
            for e, j in lastop.items():
                if e != "sp":
                    wait("sp", *sig[j])
            nc.all_engine_barrier()

        if T > 1:
            ENG = mybir.ALL_ENGINES
            regs = self.loopregs
            lid = nc.next_id()
            ls, le = "myl_%d_loop" % lid, "myl_%d_end" % lid
            nc.regs_mov(regs, 0)
            nc.br(ls, engines=ENG)
            with nc.body(ls, valid_engines=ENG):
                body(regs)
                nc.regs_alu(regs, regs, 1, op=mybir.AluOpType.add)
                nc.br_lt(regs, T, on_true=ls, on_false=le, engines=ENG)
            nc.switch_bb(le)
        else:
            body(0)
        for s, c in cnt.items():
            self.base[s] += T * c
        return n


class KB:
    def __init__(self, nc, P):
        self.nc, self.P = nc, P
        self.EN = {"dve": nc.vector, "pool": nc.gpsimd, "act": nc.scalar}

    def sb(self, st, name, shape, dt=F32):
        return st.enter_context(self.nc.sbuf_tensor(U(name), list(shape), dt)), self.P.buf()

    def psb(self, st, name, shape, dt=F32):
        return st.enter_context(self.nc.psum_tensor(U(name), list(shape), dt)), self.P.buf()

    def tt(self, eng, out, in0, in1, op, r, w):
        e = self.EN[eng]
        self.P.op(eng, lambda i: e.tensor_tensor(out=out, in0=in0, in1=in1, op=op), r, w)

    def ts(self, eng, out, in0, s1, s2, op0, op1, r, w):
        e = self.EN[eng]
        if op1 is None:
            self.P.op(eng, lambda i: e.tensor_scalar(out=out, in0=in0, scalar1=s1, scalar2=None, op0=op0), r, w)
        else:
            self.P.op(eng, lambda i: e.tensor_scalar(out=out, in0=in0, scalar1=s1, scalar2=s2, op0=op0, op1=op1), r, w)

    def stt(self, eng, out, in0, sc, in1, op0, op1, r, w):
        e = self.EN[eng]
        self.P.op(eng, lambda i: e.scalar_tensor_tensor(out=out, in0=in0, scalar=sc, in1=in1, op0=op0, op1=op1), r, w)

    def act(self, out, in_, func, r, w, bias=None, scale=None):
        kw = {}
        if bias is not None:
            kw["bias"] = bias
        if scale is not None:
            kw["scale"] = scale
        self.P.op("act", lambda i: self.nc.scalar.activation(out=out, in_=in_, func=func, **kw), r, w)

    def cp(self, eng, out, in_, r, w):
        if eng == "act":
            self.P.op("act", lambda i: self.nc.scalar.copy(out=out, in_=in_), r, w)
        else:
            e = self.EN[eng]
            self.P.op(eng, lambda i: e.tensor_copy(out=out, in_=in_), r, w)

    def ms(self, eng, ap, val, w):
        e = self.EN[eng]
        self.P.op(eng, lambda i: e.memset(ap, val), (), w)

    def mm(self, out, lhsT, rhs, start, stop, r, w):
        self.P.op("pe", lambda i: self.nc.tensor.matmul(out, lhsT=lhsT, rhs=rhs, start=start, stop=stop), r, w)

    def tr(self, out, in_, ident, r, w):
        self.P.op("pe", lambda i: self.nc.tensor.transpose(out, in_, ident), r, w)

    def ld(self, q, out, in_fn, w, r=(), cj=0):
        self.P.dma2(q, lambda i: out, in_fn, r, w, cj=(0, cj))

    def stq(self, q, out_fn, in_, r, w=(), cj=0):
        self.P.dma2(q, out_fn, lambda i: in_, r, w, cj=(cj, 0))


def bc_last(ap2, m):
    a = [list(x) for x in ap2.ap]
    return bass.AP(ap2.tensor, ap2.offset, a + [[0, m]])


C0 = 2


def seg_tiles(env, TT):
    res = []
    for (seg, col0, tok0, ntok) in env["SEGS"]:
        tn = min(TT, ntok)
        lst = [(seg, col0 + t, tok0 + t, tn) for t in range(0, ntok, tn)]
        res.append(lst)
    return res


def ln_tail(env, st, l, which, r, b_r, TT, hT, tokfn, pln, b_pln):
    nc, P, kb, cst = env["nc"], env["P"], env["kb"], env["cst"]
    ONES = cst[:, 2, :]
    psA, b_psA = kb.psb(st, "psA", [128, 512])
    psB, b_psB = kb.psb(st, "psB", [128, 512])
    sq, b_sq = kb.sb(st, "lsq", [128, TT])
    mu, b_mu = kb.sb(st, "lmu", [128, TT])
    rstd, b_rstd = kb.sb(st, "lrstd", [128, TT])
    ho = [kb.sb(st, "lho", [128, TT]) for _ in range(2)]
    for ob in range(NCH):
        kb.mm(psA[:, :TT], ONES, r[:, ob, :], ob == 0, ob == NCH - 1, [b_r], [b_psA])
        kb.tt("pool", sq[:, :], r[:, ob, :], r[:, ob, :], ALU.mult, [b_r], [b_sq])
        kb.mm(psB[:, :TT], ONES, sq[:, :], ob == 0, ob == NCH - 1, [b_sq], [b_psB])
    kb.act(mu[:, :], psA[:, :TT], AF.Copy, [b_psA], [b_mu], scale=1.0 / D)
    kb.tt("pool", sq[:, :], mu[:, :], mu[:, :], ALU.mult, [b_mu], [b_sq])
    kb.stt("dve", rstd[:, :], psB[:, :TT], 1.0 / D, sq[:, :], ALU.mult, ALU.subtract, [b_psB, b_sq], [b_rstd])
    kb.act(rstd[:, :], rstd[:, :], AF.Sqrt, [b_rstd], [b_rstd], bias=LN_EPS)
    P.op("dve", lambda i: nc.vector.reciprocal(out=rstd[:, :], in_=rstd[:, :]), [b_rstd], [b_rstd])
    for ob in range(NCH):
        ho_, b_ho = ho[ob % 2]
        kb.tt("pool", ho_[:, :], r[:, ob, :], mu[:, :], ALU.subtract, [b_r, b_mu], [b_ho])
        kb.tt("dve", ho_[:, :], ho_[:, :], rstd[:, :], ALU.mult, [b_ho, b_rstd], [b_ho])
        gcol = which * 16 + ob
        kb.ts("dve", ho_[:, :], ho_[:, :], pln[:, gcol:gcol + 1], pln[:, 32 + gcol:33 + gcol], ALU.mult, ALU.add,
              [b_ho, b_pln], [b_ho])
        kb.stq("sp", lambda i, ob=ob: hT[ob * 128:(ob + 1) * 128, tokfn(i):tokfn(i) + TT], ho_[:, :], [b_ho])


NQ = 4
LO_AX, LO_AG, LO_SB, LO_SG, LO_SX, LO_Q, LO_K, LO_V, LO_Z, LO_XBC, LO_DT = (
    0, 512, 1024, 1536, 2048, 2560, 3072, 3200, 3328, 3840, 4608)
NCOLC = 4624
DFQ = DFF // NQ
NBF = DFQ // 128
GROUPS = [[0, 1, 2, 3], [4, 5, 6, 7]]


def build(cfg):
    S = cfg["S"]
    depth = cfg["depth"]
    ST = CTX + S
    TQ = S // NQ
    L0 = C0 + CTX + 3
    WT = L0 + S + 2
    SEGS = ((0, C0, 0, CTX), (1, L0, CTX, S))
    nc = bass.Bass("TRN2", target_bir_lowering=False)
    stack = contextlib.ExitStack()
    with stack:
        P = Prog(nc, stack)
        kb = KB(nc, P)

        def din(name, shape, dt=F32):
            return nc.dram_tensor(name, list(shape), dt, kind="ExternalInput").ap()

        def dscr(name, shape, dt=F32):
            return nc.dram_tensor(name, list(shape), dt).ap()

        xc = din("xc", [D, CTX])
        CWH = min(cfg.get("cwh", 2048), TQ)
        NCC = TQ // CWH
        xq = din("xq", [NCH, NCC, 128, CWH])
        cid = din("cid", [1, 4], mybir.dt.int32)
        mod = din("mod", [2, D])
        w_ada = din("w_ada", [depth, D, 6 * D])
        w_inc = din("w_inc", [depth, D, NCOLC])
        w_mg = din("w_mg", [depth, D, 4 * D])
        consts = din("consts", [128, 10, 128])
        ropeT = din("ropeT", [2, 128, S])
        pv_lru = din("pv_lru", [depth, 4, 128, 12])
        lru_gw = din("lru_gw", [depth, 4 * 4 * 128, 128])
        pv_sc = din("pv_sc", [depth, 4, 128, 4])
        pv_att = din("pv_att", [depth, 128, 2])
        pv_ssc = din("pv_ssc", [depth, 6, 128, 8])
        pv_ssd = din("pv_ssd", [depth, 128, 4, 2])
        pv_dt = din("pv_dt", [depth, 128, 32])
        pv_ln = din("pv_ln", [depth, 128, 64])
        pv_ffn = din("pv_ffn", [depth, 2 * NBF, 128, 4])
        w_br = din("w_br", [depth, 4, D, D])
        w_out = din("w_out", [depth, D, D])
        ffn_upc = din("ffn_upc", [depth, D, 2 * DFQ])
        ffn_down = din("ffn_down", [depth, DFF, D])
        out = nc.dram_tensor("out", [NCH, NCC, 128, CWH], F32, kind="ExternalOutput").ap()
        PT = Split([(0, 2560, dscr("PTa", [2560, WT])), (2560, 4736, dscr("PTb", [4736 - 2560, WT]))])
        PTmg = dscr("PTmg", [4 * D, CTX + TQ])
        UT = Split([(0, 2 * DFQ, dscr("UTc", [2 * DFQ, WT]))])
        U3 = dscr("U3", [768, ST])
        YS = dscr("YS", [512, ST])
        BRl = [dscr("BRl%d" % j, [4, NQ, 128, TQ], BF16) for j in range(4)]
        BRx = [dscr("BRx%d" % j, [512, CTX], BF16) for j in range(4)]
        BRgl = [dscr("BRgl%d" % j, [4, NQ, NQ, 128, TQ], BF16) for j in range(4)]
        BRgx = [dscr("BRgx%d" % j, [NQ, 512, CTX], BF16) for j in range(4)]
        ACl = dscr("ACl", [NBF, NQ, 128, TQ], BF16)
        ACx = dscr("ACx", [DFQ, CTX], BF16)
        ACgl = dscr("ACgl", [NBF, NQ, NQ, 128, TQ], BF16)
        ACgx = dscr("ACgx", [NQ, DFQ, CTX], BF16)

        class BRW:
            def __init__(self, lat, ctxt):
                self.lat, self.ctxt = lat, ctxt

            def __getitem__(self, key):
                rs, cs = key
                blk = rs.start // 128
                assert rs.stop - rs.start == 128 and rs.start % 128 == 0
                if cs.start < CTX:
                    return self.ctxt[rs, cs]
                t = cs.start - CTX
                q, off = t // TQ, t % TQ
                assert off + (cs.stop - cs.start) <= TQ
                return self.lat[blk, q, :, off:off + cs.stop - cs.start]

        BRc = [BRW(BRl[j], BRx[j]) for j in range(4)]
        ACTc = BRW(ACl, ACx)
        QTg = dscr("QTg", [128, 4 * S], BF16)
        QTc = dscr("QTc", [4, 128, CTX], BF16)
        KT = dscr("KT", [128, ST], BF16)
        VTM = dscr("VTM", [ST, 128], BF16)
        hc = dscr("hc", [D, CTX])
        hq = dscr("hq", [NCH, NCC, 128, CWH])
        Hg = dscr("Hg", [NCH, NCC, NQ, 128, CWH])

        P.load_core_id(cid[0:1, 0:1])
        cst, b_cst = kb.sb(stack, "cst", [128, 10, 128])
        cstb, b_cstb = kb.sb(stack, "cstb", [128, 10, 128], BF16)
        kb.ld("sp", cst[:, :, :], lambda i: consts, [b_cst])
        kb.cp("dve", cstb[:, :, :], cst[:, :, :], [b_cst], [b_cstb])
        with contextlib.ExitStack() as st:
            z, bz = kb.sb(st, "z", [128, 8])
            kb.ms("dve", z[:, :], 0.0, [bz])
            for dst in (PT, UT):
                for (pa, pb, pap) in dst.pieces:
                    for r0 in range(0, pb - pa, 128):
                        rn = min(128, pb - pa - r0)
                        for (c0, cw) in ((0, 2), (C0 + CTX, 3), (L0 + S, 2)):
                            kb.stq("sp", lambda i, pap=pap, r0=r0, rn=rn, c0=c0, cw=cw: pap[r0:r0 + rn, c0:c0 + cw],
                                   z[:rn, :cw], [bz])
            P.emit()

        adaT, b_ada = kb.sb(stack, "adaT", [128, depth * 2 * 96])
        with contextlib.ExitStack() as st:
            smod, b_smod = kb.sb(st, "smod", [128, 2 * NCH])
            smodb, b_smodb = kb.sb(st, "smodb", [128, 2 * NCH], BF16)
            sig_t, b_sig = kb.sb(st, "sig_t", [128, 2 * NCH])
            P.dma("sp", lambda i: nc.sync.dma_start(
                out=smod[:, :].rearrange("p (s k) -> p s k", s=2),
                in_=mod.rearrange("s (k p) -> p s k", p=128), allow_slow_non_contiguous=True), w=[b_smod])
            kb.act(sig_t[:, :], smod[:, :], AF.Sigmoid, [b_smod], [b_sig])
            kb.tt("dve", smodb[:, :], smod[:, :], sig_t[:, :], ALU.mult, [b_smod, b_sig], [b_smodb])
            P.emit()
            wt = [kb.sb(st, "adw", [128, NCH, 512], BF16) for k in range(2)]
            ps = [kb.psb(st, "adps", [128, 2]) for k in range(2)]
            smv = smodb[:, :].rearrange("p (s k) -> p k s", s=2)
            adv = adaT[:, :].rearrange("p (l s k) -> p l k s", l=depth, s=2)
            for l in range(depth):
                for cb in range(6 * D // 512):
                    k = cb % 2
                    kb.ld("pool", wt[k][0][:, :, :], lambda i, l=l, cb=cb: w_ada[l, :, cb * 512:(cb + 1) * 512]
                          .rearrange("(c p) n -> p c n", p=128), [wt[k][1]])
                    for m in range(4):
                        pk = (cb * 4 + m) % 2
                        for c in range(NCH):
                            kb.mm(ps[pk][0][:, :], wt[k][0][:, c, m * 128:(m + 1) * 128], smv[:, c, :],
                                  c == 0, c == NCH - 1, [wt[k][1], b_smodb], [ps[pk][1]])
                        kb.cp("dve", adv[:, l, cb * 4 + m, :], ps[pk][0][:, :], [ps[pk][1]], [b_ada])
            P.emit()

        def ada(l, seg, which, k=None):
            col = (l * 2 + seg) * 96 + which * NCH
            if k is None:
                return adaT[:, col:col + NCH]
            return adaT[:, col + k:col + k + 1]

        with contextlib.ExitStack() as st:
            t, bt = kb.sb(st, "cp", [128, 2048])
            for k in range(NCH):
                kb.ld("sp", t[:, :CTX], lambda i, k=k: xc[k * 128:(k + 1) * 128, 0:CTX], [bt])
                kb.stq("sp", lambda i, k=k: hc[k * 128:(k + 1) * 128, 0:CTX], t[:, :CTX], [bt])
                for cc in range(NCC):
                    kb.ld("sp", t[:, :CWH], lambda i, k=k, cc=cc: xq[k, cc, :, :], [bt])
                    kb.stq("sp", lambda i, k=k, cc=cc: hq[k, cc, :, :], t[:, :CWH], [bt])
            P.emit()

        env = dict(nc=nc, P=P, kb=kb, S=S, ST=ST, TQ=TQ, L0=L0, WT=WT, SEGS=SEGS, ada=ada, cst=cst, cstb=cstb,
                   depth=depth, CWH=CWH, NCC=NCC)
        TTp = min(1024, CWH)
        all_src = [dict(seg=0, TT=CTX, T=1, src=lambda c, i: hc[c * 128:(c + 1) * 128, 0:CTX], dcol=lambda i: C0)]
        own_src = [dict(seg=0, TT=CTX, T=1, src=lambda c, i: hc[c * 128:(c + 1) * 128, 0:CTX], dcol=lambda i: 0)]
        for cc in range(NCC):
            for r in range(NQ):
                all_src.append(dict(seg=1, TT=TTp, T=CWH // TTp,
                                    src=lambda c, i, r=r, cc=cc: Hg[c, cc, r, :, i * TTp:(i + 1) * TTp],
                                    dcol=lambda i, r=r, cc=cc: L0 + r * TQ + cc * CWH + i * TTp))
            own_src.append(dict(seg=1, TT=TTp, T=CWH // TTp, src=lambda c, i, cc=cc: hq[c, cc, :, i * TTp:(i + 1) * TTp],
                                dcol=lambda i, cc=cc: CTX + cc * CWH + i * TTp))

        def gather_h():
            for c in range(NCH):
                for cc in range(NCC):
                    P.coll(hq[c, cc], Hg[c, cc], GROUPS)
            P.emit()

        for l in range(depth):
            with_ctx = l < depth - 1
            gather_h()
            proj_phase(env, l, w_inc[l], PT, NCOLC, 1, 0, all_src)
            proj_phase(env, l, w_mg[l], Split([(0, 4 * D, PTmg)]), 4 * D, 1, 0, own_src if with_ctx else own_src[1:])
            lru_phase(env, l, PT, pv_lru, lru_gw, BRc[0])
            sconv_phase(env, l, PT, pv_sc, BRc[1])
            attn_prep(env, l, PT, pv_att, ropeT, QTg, QTc, KT, VTM, with_ctx)
            attn_main(env, l, QTg, QTc, KT, VTM, BRc[2], with_ctx)
            ssd_conv(env, l, PT, pv_ssc, U3)
            ssd_main(env, l, PT, U3, pv_dt, pv_ssd, YS)
            ssd_norm(env, l, PT, YS, pv_ssd, BRc[3])
            for j in range(4):
                for rb in range(4):
                    for q in range(NQ):
                        P.coll(BRl[j][rb, q], BRgl[j][rb, q], GROUPS)
                if with_ctx:
                    P.coll(BRx[j], BRgx[j], GROUPS)
            P.emit()
            merge_phase(env, l, PTmg, BRgl, BRgx, w_br, w_out, hc, hq, pv_ln, with_ctx)
            gather_h()
            proj_phase(env, l, ffn_upc[l], UT, 2 * DFQ, 4, 3, all_src if with_ctx else all_src[1:])
            ffn_act(env, l, UT, pv_ffn, ACTc, with_ctx)
            for rb in range(NBF):
                for q in range(NQ):
                    P.coll(ACl[rb, q], ACgl[rb, q], GROUPS)
            if with_ctx:
                P.coll(ACx, ACgx, GROUPS)
            P.emit()
            ffn_down_phase(env, l, ACgl, ACgx, ffn_down, hc, hq, pv_ln, with_ctx)

        with contextlib.ExitStack() as st:
            t, bt = kb.sb(st, "cpo", [128, 2048])
            for k in range(NCH):
                for cc in range(NCC):
                    kb.ld("sp", t[:, :CWH], lambda i, k=k, cc=cc: hq[k, cc, :, :], [bt])
                    kb.stq("sp", lambda i, k=k, cc=cc: out[k, cc, :, :], t[:, :CWH], [bt])
            P.emit()
    return nc


def proj_phase(env, l, w, dst, ncol, sc_which, sh_which, sources):
    nc, P, kb, ada = env["nc"], env["P"], env["kb"], env["ada"]
    NB = (ncol + 127) // 128
    for sd in sources:
        seg, TT, T, src, dcol = sd["seg"], sd["TT"], sd["T"], sd["src"], sd["dcol"]
        with contextlib.ExitStack() as st:
            NH = (TT + 511) // 512
            HW = TT // NH
            xf, b_xf = kb.sb(st, "xf", [128, TT])
            xm = st.enter_context(nc.sbuf_tensor(U("xm"), [128, NCH, TT], BF16))
            b_xm = [P.buf() for _ in range(NCH)]
            sc1p, b_sc = kb.sb(st, "sc1p", [128, NCH])
            wt = [kb.sb(st, "wt", [128, NCH, 512], BF16) for k in range(3)]
            ev = [kb.sb(st, "ev", [128, TT]) for k in range(3)]
            ps = [kb.psb(st, "ps", [128, 512]) for k in range(6)]
            kb.ts("dve", sc1p[:, :], ada(l, seg, sc_which), 1.0, None, ALU.add, None, [], [b_sc])
            P.emit()
            for c in range(NCH):
                kb.ld("sp", xf[:, :], lambda i, c=c: src(c, i), [b_xf])
                kb.ts("dve", xm[:, c, :], xf[:, :], sc1p[:, c:c + 1], ada(l, seg, sh_which, c), ALU.mult, ALU.add,
                      [b_xf, b_sc], [b_xm[c]])
            pi = 0
            for g in range((NB + 3) // 4):
                k = g % 3
                c0 = g * 512
                cw = min(512, ncol - c0)
                kb.ld("pool", wt[k][0][:, :, :cw], lambda i, c0=c0, cw=cw: w[:, c0:c0 + cw]
                      .rearrange("(c p) n -> p c n", p=128), [wt[k][1]])
                for m in range((cw + 127) // 128):
                    mw = min(128, cw - m * 128)
                    e = (g * 4 + m) % 3
                    for hh in range(NH):
                        p_ = pi % 6
                        pi += 1
                        for c in range(NCH):
                            kb.mm(ps[p_][0][:mw, :HW], wt[k][0][:, c, m * 128:m * 128 + mw],
                                  xm[:, c, hh * HW:(hh + 1) * HW], c == 0, c == NCH - 1,
                                  [wt[k][1]] + b_xm, [ps[p_][1]])
                        kb.cp("act" if pi % 2 == 0 else "dve", ev[e][0][:mw, hh * HW:(hh + 1) * HW],
                              ps[p_][0][:mw, :HW], [ps[p_][1]], [ev[e][1]])
                    row0 = c0 + m * 128
                    kb.stq("sp", lambda i, mw=mw, row0=row0: dst[row0:row0 + mw, dcol(i):dcol(i) + TT],
                           ev[e][0][:mw, :], [ev[e][1]])
            P.emit(T)
def lru_phase(env, l, PT, pv_lru, lru_gw, BR):
    nc, P, kb, S, ST = env["nc"], env["P"], env["kb"], env["S"], env["ST"]
    TT = min(1024, env["TQ"])
    tiles = seg_tiles(env, TT)
    pvv = pv_lru[l].rearrange("b p k -> (b p) k")
    with contextlib.ExitStack() as st:
        pv, b_pv = kb.sb(st, "pv", [128, 12])
        gw = [kb.sb(st, "gw", [128, 128], BF16) for _ in range(4)]
        e1, b_e1 = kb.sb(st, "e1", [128, 2])
        m8, b_m8 = kb.sb(st, "m8", [128, 2])
        state, b_state = kb.sb(st, "state", [128, 1])
        yacc, b_yacc = kb.sb(st, "yacc", [128, ST])
        xh, b_xh = kb.sb(st, "xh", [128, TT + 3])
        u, b_u = kb.sb(st, "u", [128, TT])
        ub, b_ub = kb.sb(st, "ub", [128, TT], BF16)
        rr, b_rr = kb.sb(st, "rr", [128, TT])
        ii, b_ii = kb.sb(st, "ii", [128, TT])
        aa, b_aa = kb.sb(st, "aa", [128, TT])
        t1, b_t1 = kb.sb(st, "t1", [128, TT])
        t2, b_t2 = kb.sb(st, "t2", [128, TT])
        gt, b_gt = kb.sb(st, "gt", [128, TT])
        ob, b_ob = kb.sb(st, "ob", [128, TT], BF16)
        ps = [kb.psb(st, "psl", [128, 512]) for _ in range(4)]
        kb.ld("sp", pv[:, :], lambda i: pvv[i * 128:(i + 1) * 128, :], [b_pv])
        for k in range(4):
            kb.ld("pool", gw[k][0][:, :], lambda i, k=k: lru_gw[l, i * 128 + k * 512:i * 128 + k * 512 + 128, :], [gw[k][1]])
        kb.act(e1[:, :], pv[:, 9:11], AF.Exp, [b_pv], [b_e1], scale=-1.0)
        kb.act(e1[:, :], e1[:, :], AF.Ln, [b_e1], [b_e1], bias=1.0)
        kb.ts("dve", m8[:, :], e1[:, :], -8.0, None, ALU.mult, None, [b_e1], [b_m8])
        pi = 0
        for d in range(2):
            kb.ms("dve", state[:, :], 0.0, [b_state])
            for sg in tiles:
                for (seg, col, tok, tn) in (sg if d == 0 else sg[::-1]):
                    NH = (tn + 511) // 512
                    HW = tn // NH
                    kb.ld("sp", xh[:, :tn + 3], lambda i, col=col, tn=tn: PT[i * 128 + LO_AX:i * 128 + LO_AX + 128, col - 2:col + tn + 1], [b_xh])
                    kb.ts("dve", u[:, :tn], xh[:, 0:tn], pv[:, 0:1], pv[:, 4:5], ALU.mult, ALU.add, [b_xh, b_pv], [b_u])
                    for k in range(1, 4):
                        kb.stt("dve", u[:, :tn], xh[:, k:k + tn], pv[:, k:k + 1], u[:, :tn], ALU.mult, ALU.add,
                               [b_xh, b_pv, b_u], [b_u])
                    kb.cp("act", ub[:, :tn], u[:, :tn], [b_u], [b_ub])
                    for wh, (dst, b_dst) in enumerate(((rr, b_rr), (ii, b_ii))):
                        for hh in range(NH):
                            p_ = pi % 4
                            pi += 1
                            kb.mm(ps[p_][0][:, :HW], gw[d * 2 + wh][0][:, :], ub[:, hh * HW:(hh + 1) * HW], True, True,
                                  [gw[d * 2 + wh][1], b_ub], [ps[p_][1]])
                            kb.act(dst[:, hh * HW:(hh + 1) * HW], ps[p_][0][:, :HW], AF.Sigmoid, [ps[p_][1], b_pv], [b_dst],
                                   bias=pv[:, 5 + d * 2 + wh:6 + d * 2 + wh])
                    kb.act(aa[:, :tn], rr[:, :tn], AF.Exp, [b_rr, b_m8], [b_aa], scale=m8[:, d:d + 1])
                    kb.stt("dve", t1[:, :tn], aa[:, :tn], -1.0, aa[:, :tn], ALU.mult, ALU.mult, [b_aa], [b_t1])
                    kb.act(t1[:, :tn], t1[:, :tn], AF.Sqrt, [b_t1], [b_t1], bias=1.0)
                    kb.tt("pool", t2[:, :tn], ii[:, :tn], u[:, :tn], ALU.mult, [b_ii, b_u], [b_t2])
                    kb.tt("dve", t2[:, :tn], t2[:, :tn], t1[:, :tn], ALU.mult, [b_t1, b_t2], [b_t2])
                    if d == 0:
                        o_ap = yacc[:, tok:tok + tn]
                        P.op("dve", lambda i, o_ap=o_ap, tn=tn: nc.vector.tensor_tensor_scan(
                            out=o_ap, data0=aa[:, :tn], data1=t2[:, :tn], initial=state[:, 0:1],
                            op0=ALU.mult, op1=ALU.add), [b_aa, b_t2, b_state], [b_yacc])
                        kb.cp("dve", state[:, :], yacc[:, tok + tn - 1:tok + tn], [b_yacc], [b_state])
                    else:
                        P.op("dve", lambda i, tn=tn: nc.vector.tensor_tensor_scan(
                            out=rr[:, tn - 1::-1] if False else rr[:, 0:tn][:, ::-1], data0=aa[:, 0:tn][:, ::-1],
                            data1=t2[:, 0:tn][:, ::-1], initial=state[:, 0:1],
                            op0=ALU.mult, op1=ALU.add), [b_aa, b_t2, b_state], [b_rr])
                        kb.cp("dve", state[:, :], rr[:, 0:1], [b_rr], [b_state])
                        kb.tt("dve", rr[:, :tn], rr[:, :tn], yacc[:, tok:tok + tn], ALU.add, [b_rr, b_yacc], [b_rr])
                        kb.ld("sp", gt[:, :tn], lambda i, col=col, tn=tn: PT[i * 128 + LO_AG:i * 128 + LO_AG + 128, col:col + tn], [b_gt])
                        kb.tt("pool", t1[:, :tn], gt[:, :tn], gt[:, :tn], ALU.mult, [b_gt], [b_t1])
                        kb.ts("dve", t1[:, :tn], t1[:, :tn], 0.044715, 1.0, ALU.mult, ALU.add, [b_t1], [b_t1])
                        kb.tt("dve", t1[:, :tn], t1[:, :tn], gt[:, :tn], ALU.mult, [b_t1, b_gt], [b_t1])
                        kb.act(t1[:, :tn], t1[:, :tn], AF.Sigmoid, [b_t1], [b_t1], scale=1.5957691216057308)
                        kb.tt("pool", t1[:, :tn], t1[:, :tn], gt[:, :tn], ALU.mult, [b_t1, b_gt], [b_t1])
                        kb.tt("dve", ob[:, :tn], t1[:, :tn], rr[:, :tn], ALU.mult, [b_t1, b_rr], [b_ob])
                        kb.stq("sp", lambda i, tok=tok, tn=tn: BR[i * 128:(i + 1) * 128, tok:tok + tn], ob[:, :tn], [b_ob])
        P.emit(4)


def sconv_phase(env, l, PT, pv_sc, BR):
    nc, P, kb, S, ST = env["nc"], env["P"], env["kb"], env["S"], env["ST"]
    TT = min(1024, env["TQ"])
    tiles = seg_tiles(env, TT)
    pvv = pv_sc[l].rearrange("b p k -> (b p) k")
    with contextlib.ExitStack() as st:
        pv, b_pv = kb.sb(st, "pvs", [128, 4])
        sbt, b_sbt = kb.sb(st, "sbt", [128, TT])
        sgh, b_sgh = kb.sb(st, "sgh", [128, TT + 2])
        sxh, b_sxh = kb.sb(st, "sxh", [128, TT + 2])
        o, b_o = kb.sb(st, "o", [128, TT])
        ob, b_ob = kb.sb(st, "obs", [128, TT], BF16)
        kb.ld("sp", pv[:, :], lambda i: pvv[i * 128:(i + 1) * 128, :], [b_pv])
        for sg in tiles:
            for (seg, col, tok, tn) in sg:
                kb.ld("sp", sbt[:, :tn], lambda i, col=col, tn=tn: PT[i * 128 + LO_SB:i * 128 + LO_SB + 128, col:col + tn], [b_sbt])
                kb.ld("sp", sgh[:, :tn + 2], lambda i, col=col, tn=tn: PT[i * 128 + LO_SG:i * 128 + LO_SG + 128, col - 1:col + tn + 1], [b_sgh])
                kb.ld("sp", sxh[:, :tn + 2], lambda i, col=col, tn=tn: PT[i * 128 + LO_SX:i * 128 + LO_SX + 128, col - 1:col + tn + 1], [b_sxh])
                kb.tt("pool", sgh[:, :tn + 2], sgh[:, :tn + 2], sxh[:, :tn + 2], ALU.mult, [b_sgh, b_sxh], [b_sgh])
                kb.ts("dve", o[:, :tn], sgh[:, 0:tn], pv[:, 0:1], None, ALU.mult, None, [b_sgh, b_pv], [b_o])
                for k in (1, 2):
                    kb.stt("dve", o[:, :tn], sgh[:, k:k + tn], pv[:, k:k + 1], o[:, :tn], ALU.mult, ALU.add,
                           [b_sgh, b_pv, b_o], [b_o])
                kb.tt("dve", ob[:, :tn], o[:, :tn], sbt[:, :tn], ALU.mult, [b_o, b_sbt], [b_ob])
                kb.stq("sp", lambda i, tok=tok, tn=tn: BR[i * 128:(i + 1) * 128, tok:tok + tn], ob[:, :tn], [b_ob])
        P.emit(4)


def attn_prep(env, l, PT, pv_att, ropeT, QTg, QTc, KT, VTM, with_ctx):
    nc, P, kb, S, ST, cst = env["nc"], env["P"], env["kb"], env["S"], env["ST"], env["cst"]
    IDENT, RMAT, ONES = cst[:, 0, :], cst[:, 1, :], cst[:, 2, :]
    for (seg, col0, tok0, ntok) in env["SEGS"]:
        TT = min(512, ntok)
        T = ntok // TT
        NS = TT // 128
        lat = seg == 1
        with contextlib.ExitStack() as st:
            pv, b_pv = kb.sb(st, "pva", [128, 2])
            cs, b_cs = kb.sb(st, "cs", [128, TT])
            sn, b_sn = kb.sb(st, "sn", [128, TT])
            qf = [kb.sb(st, "qf", [128, TT]) for _ in range(2)]
            sq, b_sq = kb.sb(st, "sq", [128, TT])
            rs, b_rs = kb.sb(st, "rs", [128, TT])
            qn, b_qn = kb.sb(st, "qn", [128, TT])
            t1, b_t1 = kb.sb(st, "t1a", [128, TT])
            t2, b_t2 = kb.sb(st, "t2a", [128, TT])
            qo = [kb.sb(st, "qo", [128, TT], BF16) for _ in range(2)]
            vb, b_vb = kb.sb(st, "vb", [128, NS, 128], BF16)
            ps1, b_ps1 = kb.psb(st, "pa1", [128, 512])
            ps2, b_ps2 = kb.psb(st, "pa2", [128, 512])
            pst = [kb.psb(st, "pat", [128, 128]) for _ in range(2)]
            kb.ld("sp", pv[:, :], lambda i: pv_att[l], [b_pv])
            if lat:
                kb.ld("sp", cs[:, :], lambda i: ropeT[0, :, i * TT:(i + 1) * TT], [b_cs])
                kb.ld("sp", sn[:, :], lambda i: ropeT[1, :, i * TT:(i + 1) * TT], [b_sn])
            heads = [("q", h) for h in range(4)] if (lat or with_ctx) else []
            heads += [("k", 0)]
            for n_, (kind, h) in enumerate(heads):
                row = (LO_Q if kind == "q" else LO_K) + h * 128
                qf_, b_qf = qf[n_ % 2]
                qo_, b_qo = qo[n_ % 2]
                kb.ld("sp", qf_[:, :], lambda i, row=row: PT[row:row + 128, i * TT + col0:i * TT + col0 + TT], [b_qf])
                kb.tt("pool", sq[:, :], qf_[:, :], qf_[:, :], ALU.mult, [b_qf], [b_sq])
                kb.mm(ps1[:, :TT], ONES, sq[:, :], True, True, [b_sq], [b_ps1])
                kb.act(rs[:, :], ps1[:, :TT], AF.Sqrt, [b_ps1], [b_rs], bias=RMS_EPS, scale=1.0 / 128)
                P.op("dve", lambda i: nc.vector.reciprocal(out=rs[:, :], in_=rs[:, :]), [b_rs], [b_rs])
                gcol = pv[:, 0:1] if kind == "q" else pv[:, 1:2]
                if lat:
                    kb.stt("dve", qn[:, :], qf_[:, :], gcol, rs[:, :], ALU.mult, ALU.mult, [b_qf, b_pv, b_rs], [b_qn])
                    kb.mm(ps2[:, :TT], RMAT, qn[:, :], True, True, [b_qn], [b_ps2])
                    kb.tt("pool", t1[:, :], qn[:, :], cs[:, :], ALU.mult, [b_qn, b_cs], [b_t1])
                    kb.tt("dve", t2[:, :], ps2[:, :TT], sn[:, :], ALU.mult, [b_ps2, b_sn], [b_t2])
                    kb.tt("dve", qo_[:, :], t1[:, :], t2[:, :], ALU.add, [b_t1, b_t2], [b_qo])
                else:
                    kb.stt("dve", qo_[:, :], qf_[:, :], gcol, rs[:, :], ALU.mult, ALU.mult, [b_qf, b_pv, b_rs], [b_qo])
                if kind == "q" and lat:
                    kb.stq("sp", lambda i, h=h: QTg[:, i * TT + h * S:i * TT + h * S + TT], qo_[:, :], [b_qo])
                elif kind == "q":
                    kb.stq("sp", lambda i, h=h: QTc[h, :, :], qo_[:, :], [b_qo])
                else:
                    kb.stq("sp", lambda i, h=h: KT[:, i * TT + tok0:i * TT + tok0 + TT], qo_[:, :], [b_qo])
            for h in range(1):
                qf_, b_qf = qf[h % 2]
                kb.ld("sp", qf_[:, :], lambda i, h=h: PT[LO_V + h * 128:LO_V + (h + 1) * 128, i * TT + col0:i * TT + col0 + TT], [b_qf])
                for s_ in range(NS):
                    pt_, b_pt = pst[(h * NS + s_) % 2]
                    kb.tr(pt_[:, :], qf_[:, s_ * 128:(s_ + 1) * 128], IDENT, [b_qf], [b_pt])
                    kb.cp("act" if s_ % 2 else "dve", vb[:, s_, h * 128:(h + 1) * 128], pt_[:, :], [b_pt], [b_vb])
            kb.stq("sp", lambda i: VTM[i * TT + tok0:i * TT + tok0 + TT, :].rearrange("(s p) c -> p s c", p=128), vb[:, :, :], [b_vb])
            P.emit(T)


def attn_main(env, l, QTg, QTc, KT, VTM, BR2, with_ctx):
    nc, P, kb, S, ST, cstb = env["nc"], env["P"], env["kb"], env["S"], env["ST"], env["cstb"]
    ONESB = cstb[:, 2, :]
    NKC = ST // 128
    SCALE = 128.0 ** -0.5
    NQT = S // 512
    with contextlib.ExitStack() as st:
        Kt, b_Kt = kb.sb(st, "Kt", [128, ST], BF16)
        Vt, b_Vt = kb.sb(st, "Vt", [128, NKC, 128], BF16)
        Qt = [kb.sb(st, "Qt", [128, 512], BF16) for _ in range(2)]
        pt = [kb.sb(st, "pt", [128, 512], BF16) for _ in range(4)]
        rden, b_rden = kb.sb(st, "rden", [128, 512])
        ob, b_ob = kb.sb(st, "oba", [128, 512], BF16)
        pss = [kb.psb(st, "pss", [128, 512]) for _ in range(3)]
        pso, b_pso = kb.psb(st, "pso", [128, 512])
        psd, b_psd = kb.psb(st, "psd", [128, 512])
        kb.ld("sp", Kt[:, :], lambda i: KT, [b_Kt])
        kb.ld("sp", Vt[:, :, :], lambda i: VTM.rearrange("(c p) d -> p c d", p=128), [b_Vt])
        P.emit()

        def attend(qw, kcs, out_fn, q_fn):
            Qt_, b_Qt = Qt[0]
            kb.ld("sp", Qt_[:, :qw], q_fn, [b_Qt])
            n = len(kcs)

            def s_mm(n_):
                ps_, b_ps = pss[n_ % 3]
                kc = kcs[n_]
                kb.mm(ps_[:, :qw], Kt[:, kc * 128:(kc + 1) * 128], Qt_[:, :qw], True, True, [b_Kt, b_Qt], [b_ps])

            s_mm(0)
            for n_, kc in enumerate(kcs):
                ps_, b_ps = pss[n_ % 3]
                pt_, b_pt = pt[n_ % 4]
                if n_ + 1 < n:
                    s_mm(n_ + 1)
                kb.act(pt_[:, :qw], ps_[:, :qw], AF.Exp, [b_ps], [b_pt], scale=SCALE)
                kb.mm(pso[:, :qw], Vt[:, kc, :], pt_[:, :qw], n_ == 0, n_ == n - 1, [b_Vt, b_pt], [b_pso])
                kb.mm(psd[:, :qw], ONESB, pt_[:, :qw], n_ == 0, n_ == n - 1, [b_pt], [b_psd])
            P.op("dve", lambda i: nc.vector.reciprocal(out=rden[:, :qw], in_=psd[:, :qw]), [b_psd], [b_rden])
            kb.tt("dve", ob[:, :qw], pso[:, :qw], rden[:, :qw], ALU.mult, [b_pso, b_rden], [b_ob])
            kb.stq("sp", out_fn, ob[:, :qw], [b_ob])

        TQ = env["TQ"]
        QW = min(512, TQ)
        for qh in range(4):
            for qq in range(NQ):
                attend(QW, list(range(NKC)),
                       lambda i, qh=qh, qq=qq: BR2[qh * 128:(qh + 1) * 128, CTX + qq * TQ + i * QW:CTX + qq * TQ + (i + 1) * QW],
                       lambda i, qh=qh, qq=qq: QTg[:, qh * S + qq * TQ + i * QW:qh * S + qq * TQ + (i + 1) * QW])
                P.emit(TQ // QW)
        if with_ctx:
            for qh in range(4):
                attend(CTX, list(range(CTX // 128)), lambda i, qh=qh: BR2[qh * 128:(qh + 1) * 128, 0:CTX],
                       lambda i, qh=qh: QTc[qh, :, :])
            P.emit()


def ssd_conv(env, l, PT, pv_ssc, U3):
    nc, P, kb, S = env["nc"], env["P"], env["kb"], env["S"]
    TT = min(1024, env["TQ"])
    tiles = seg_tiles(env, TT)
    pvv = pv_ssc[l].rearrange("b p k -> (b p) k")
    with contextlib.ExitStack() as st:
        pv, b_pv = kb.sb(st, "pvc", [128, 8])
        xh, b_xh = kb.sb(st, "xhc", [128, TT + 3])
        u, b_u = kb.sb(st, "uc", [128, TT])
        kb.ld("sp", pv[:, :], lambda i: pvv[i * 128:(i + 1) * 128, :], [b_pv])
        for sg in tiles:
            for (seg, col, tok, tn) in sg:
                kb.ld("sp", xh[:, :tn + 3], lambda i, col=col, tn=tn: PT[i * 128 + LO_XBC:i * 128 + LO_XBC + 128, col - 2:col + tn + 1], [b_xh])
                kb.ts("dve", u[:, :tn], xh[:, 0:tn], pv[:, 0:1], pv[:, 4:5], ALU.mult, ALU.add, [b_xh, b_pv], [b_u])
                for k in range(1, 4):
                    kb.stt("dve", u[:, :tn], xh[:, k:k + tn], pv[:, k:k + 1], u[:, :tn], ALU.mult, ALU.add,
                           [b_xh, b_pv, b_u], [b_u])
                kb.act(u[:, :tn], u[:, :tn], AF.Silu, [b_u], [b_u])
                kb.stq("sp", lambda i, tok=tok, tn=tn: U3[i * 128:(i + 1) * 128, tok:tok + tn], u[:, :tn], [b_u])
        P.emit(6)


def ssd_main(env, l, PT, U3, pv_dt, pv_ssd, YS):
    nc, P, kb, S, cst = env["nc"], env["P"], env["kb"], env["S"], env["cst"]
    IDENT, ONES = cst[:, 0, :], cst[:, 2, :]
    NHD, NXB = 8, 4
    with contextlib.ExitStack() as st0:
        pvd, b_pvd = kb.sb(st0, "pvd", [128, 32])
        aneg, b_aneg = kb.sb(st0, "aneg", [128, 16])
        dsk, b_dsk = kb.sb(st0, "dsk", [128, NXB, 2])
        H, b_H = kb.sb(st0, "H", [128, 512])
        kb.ld("sp", pvd[:, :], lambda i: pv_dt[l], [b_pvd])
        kb.ld("sp", dsk[:, :, :], lambda i: pv_ssd[l], [b_dsk])
        kb.act(aneg[:, :], pvd[:, 16:32], AF.Exp, [b_pvd], [b_aneg])
        kb.ts("dve", aneg[:, :], aneg[:, :], -1.0, None, ALU.mult, None, [b_aneg], [b_aneg])
        P.emit()
        for d in range(2):
            TRI, MASK, NEG = (cst[:, 3, :], cst[:, 4, :], cst[:, 5, :]) if d == 0 else (cst[:, 6, :], cst[:, 7, :], cst[:, 8, :])
            kb.ms("dve", H[:, :], 0.0, [b_H])
            P.emit()
            for (seg, col0, tok0, ntok) in env["SEGS"]:
                T = ntok // 128
                with contextlib.ExitStack() as st:
                    xT_, b_xT = kb.sb(st, "xT_", [128, NXB, 128])
                    BT_, b_BT = kb.sb(st, "BT_", [128, 128])
                    CT_, b_CT = kb.sb(st, "CT_", [128, 128])
                    dtT, b_dtT = kb.sb(st, "dtT", [16, 128])
                    yprev, b_yprev = kb.sb(st, "yprev", [128, NXB, 128])
                    dtv, b_dtv = kb.sb(st, "dtv", [128, NHD])
                    av, b_av = kb.sb(st, "av", [128, NHD])
                    acs, b_acs = kb.sb(st, "acs", [128, NHD])
                    dec, b_dec = kb.sb(st, "dec", [128, NHD])
                    cdb, b_cdb = kb.sb(st, "cdb", [128, NHD])
                    xx, b_xx = kb.sb(st, "xx", [128, 512], BF16)
                    xxd, b_xxd = kb.sb(st, "xxd", [128, 512], BF16)
                    Btm, b_Btm = kb.sb(st, "Btm", [128, 128], BF16)
                    Bb, b_Bb = kb.sb(st, "Bb", [128, 128], BF16)
                    Cb, b_Cb = kb.sb(st, "Cb", [128, 128], BF16)
                    Gs, b_Gs = kb.sb(st, "Gs", [128, 128])
                    A1 = [kb.sb(st, "A1", [128, 128]) for _ in range(2)]
                    Ab = [kb.sb(st, "Ab", [128, 128]) for _ in range(2)]
                    Ee = [kb.sb(st, "Ee", [128, 128]) for _ in range(2)]
                    EA = [kb.sb(st, "EA", [128, 128]) for _ in range(2)]
                    Mb = [kb.sb(st, "Mb", [128, 128], BF16) for _ in range(2)]
                    Cs = [kb.sb(st, "Cs", [128, 128]) for _ in range(2)]
                    ysb, b_ysb = kb.sb(st, "ysb", [128, NXB, 128])
                    psx, b_psx = kb.psb(st, "psx", [128, 512])
                    psm, b_psm = kb.psb(st, "psm", [128, 128])
                    psb_, b_psb = kb.psb(st, "psbb", [128, 256])
                    psDE_t = st.enter_context(nc.psum_tensor(U("psDE"), [128, 512], F32))
                    psDE = [(psDE_t[:, 0:256], P.buf()), (psDE_t[:, 256:512], P.buf())]
                    psY = st.enter_context(nc.psum_tensor(U("psY"), [128, 512], F32))
                    b_psY = [P.buf() for _ in range(4)]

                    def tokf(i):
                        return tok0 + (i if d == 0 else T - 1 - i) * 128

                    def colf(i):
                        return col0 + (i if d == 0 else T - 1 - i) * 128

                    kb.ld("sp", xT_[:, :, :], lambda i: U3[0:512, tokf(i):tokf(i) + 128].rearrange("(b p) t -> p b t", p=128), [b_xT])
                    kb.ld("sp", BT_[:, :], lambda i: U3[512:640, tokf(i):tokf(i) + 128], [b_BT])
                    kb.ld("sp", CT_[:, :], lambda i: U3[640:768, tokf(i):tokf(i) + 128], [b_CT])
                    kb.ld("sp", dtT[:, :], lambda i: PT[LO_DT:LO_DT + 16, colf(i):colf(i) + 128], [b_dtT])
                    if d == 1:
                        kb.ld("sp", yprev[:, :, :], lambda i: YS[:, tokf(i):tokf(i) + 128].rearrange("(b p) t -> p b t", p=128), [b_yprev])
                    kb.tr(psm[:, 0:16], dtT[:, :], IDENT[0:16, 0:16], [b_dtT], [b_psm])
                    kb.tt("dve", dtv[:, :], psm[:, d * 8:(d + 1) * 8], pvd[:, d * 8:(d + 1) * 8], ALU.add, [b_psm, b_pvd], [b_dtv])
                    kb.act(dtv[:, :], dtv[:, :], AF.Exp, [b_dtv], [b_dtv])
                    kb.act(dtv[:, :], dtv[:, :], AF.Ln, [b_dtv], [b_dtv], bias=1.0)
                    kb.tt("dve", av[:, :], dtv[:, :], aneg[:, d * 8:(d + 1) * 8], ALU.mult, [b_dtv, b_aneg], [b_av])
                    kb.mm(psm[:, 64:72], TRI, av[:, :], True, True, [b_av], [b_psm])
                    kb.mm(psm[:, 96:104], ONES, av[:, :], True, True, [b_av], [b_psm])
                    kb.cp("act", acs[:, :], psm[:, 64:72], [b_psm], [b_acs])
                    kb.tt("dve", dec[:, :], psm[:, 96:104], acs[:, :], ALU.subtract, [b_psm, b_acs], [b_dec])
                    kb.act(dec[:, :], dec[:, :], AF.Exp, [b_dec], [b_dec])
                    kb.act(cdb[:, :], psm[:, 96:104], AF.Exp, [b_psm], [b_cdb])
                    for b in range(NXB):
                        kb.tr(psx[:, b * 128:(b + 1) * 128], xT_[:, b, :], IDENT, [b_xT], [b_psx])
                    kb.tt("dve", xx[:, :].rearrange("p (h e) -> p h e", e=64),
                          psx[:, :].rearrange("p (h e) -> p h e", e=64), bc_last(dtv[:, :], 64),
                          ALU.mult, [b_psx, b_dtv], [b_xx])
                    kb.tt("pool", xxd[:, :].rearrange("p (h e) -> p h e", e=64),
                          xx[:, :].rearrange("p (h e) -> p h e", e=64), bc_last(dec[:, :], 64),
                          ALU.mult, [b_xx, b_dec], [b_xxd])
                    kb.tr(psb_[:, 0:128], BT_[:, :], IDENT, [b_BT], [b_psb])
                    kb.cp("act", Btm[:, :], psb_[:, 0:128], [b_psb], [b_Btm])
                    kb.cp("pool", Bb[:, :], BT_[:, :], [b_BT], [b_Bb])
                    kb.cp("pool", Cb[:, :], CT_[:, :], [b_CT], [b_Cb])
                    kb.mm(psb_[:, 128:256], Bb[:, :], Cb[:, :], True, True, [b_Bb, b_Cb], [b_psb])
                    kb.cp("act", Gs[:, :], psb_[:, 128:256], [b_psb], [b_Gs])
                    for h in range(NHD):
                        k = h % 2
                        kb.ts("dve", A1[k][0][:, :], MASK, av[:, h:h + 1], None, ALU.mult, None, [b_av], [A1[k][1]])
                        kb.ts("pool", Ab[k][0][:, :], ONES, av[:, h:h + 1], None, ALU.mult, None, [b_av], [Ab[k][1]])
                        pD, b_pD = psDE[k]
                        kb.mm(pD[:, 0:128], A1[k][0][:, :], TRI, True, False, [A1[k][1]], [b_pD])
                        kb.mm(pD[:, 0:128], IDENT, NEG, False, True, [], [b_pD])
                        kb.mm(pD[:, 128:256], Ab[k][0][:, :], TRI, True, True, [Ab[k][1]], [b_pD])
                        kb.act(Ee[k][0][:, :], pD[:, 0:128], AF.Exp, [b_pD], [Ee[k][1]])
                        kb.act(EA[k][0][:, :], pD[:, 128:256], AF.Exp, [b_pD], [EA[k][1]])
                        kb.tt("dve", Mb[k][0][:, :], Ee[k][0][:, :], Gs[:, :], ALU.mult, [Ee[k][1], b_Gs], [Mb[k][1]])
                        kb.tt("pool", Cs[k][0][:, :], CT_[:, :], EA[k][0][:, :], ALU.mult, [EA[k][1], b_CT], [Cs[k][1]])
                        slot = (h // 2) % 4
                        yo = psY[(h % 2) * 64:(h % 2) * 64 + 64, slot * 128:(slot + 1) * 128]
                        kb.mm(yo, xx[:, h * 64:(h + 1) * 64], Mb[k][0][:, :], True, False, [b_xx, Mb[k][1]], [b_psY[slot]])
                        kb.mm(yo, H[:, h * 64:(h + 1) * 64], Cs[k][0][:, :], False, True, [b_H, Cs[k][1]], [b_psY[slot]])
                        if h % 2 == 1:
                            b = h // 2
                            ysl = psY[:, slot * 128:(slot + 1) * 128]
                            if d == 0:
                                kb.stt("dve", ysb[:, b, :], xT_[:, b, :], dsk[:, b, 1:2], ysl, ALU.mult, ALU.add,
                                       [b_xT, b_dsk, b_psY[slot]], [b_ysb])
                            else:
                                kb.tt("dve", ysb[:, b, :], ysl, yprev[:, b, :], ALU.add, [b_psY[slot], b_yprev], [b_ysb])
                    kb.stq("sp", lambda i: YS[:, tokf(i):tokf(i) + 128].rearrange("(b p) t -> p b t", p=128), ysb[:, :, :], [b_ysb])
                    kb.mm(psx[:, :], Btm[:, :], xxd[:, :], True, True, [b_Btm, b_xxd], [b_psx])
                    kb.tt("pool", H[:, :].rearrange("p (h e) -> p h e", e=64), H[:, :].rearrange("p (h e) -> p h e", e=64),
                          bc_last(cdb[:, :], 64), ALU.mult, [b_H, b_cdb], [b_H])
                    kb.tt("dve", H[:, :], H[:, :], psx[:, :], ALU.add, [b_H, b_psx], [b_H])
                    P.emit(T)


def ssd_norm(env, l, PT, YS, pv_ssd, BR3):
    nc, P, kb, S, cst = env["nc"], env["P"], env["kb"], env["S"], env["cst"]
    ONES = cst[:, 2, :]
    TQ = env["TQ"]
    segs = [env["SEGS"][0]] + [(1, env["L0"] + q * TQ, CTX + q * TQ, TQ) for q in range(NQ)]
    for (seg, col0, tok0, ntok) in segs:
        TT = min(512, ntok)
        T = ntok // TT
        with contextlib.ExitStack() as st:
            dsk, b_dsk = kb.sb(st, "dskn", [128, 4, 2])
            yg, b_yg = kb.sb(st, "yg", [128, 4, TT])
            y, b_y = kb.sb(st, "yn", [128, TT])
            z, b_z = kb.sb(st, "zn", [128, TT])
            sq, b_sq = kb.sb(st, "sqn", [128, TT])
            rs, b_rs = kb.sb(st, "rsn", [128, TT])
            ob, b_ob = kb.sb(st, "obn", [128, TT], BF16)
            ps, b_ps = kb.psb(st, "psn", [128, 512])
            kb.ld("sp", dsk[:, :, :], lambda i: pv_ssd[l], [b_dsk])
            for b in range(4):
                kb.ld("sp", y[:, :], lambda i, b=b: YS[b * 128:(b + 1) * 128, i * TT + tok0:i * TT + tok0 + TT], [b_y])
                kb.ld("sp", z[:, :], lambda i, b=b: PT[LO_Z + b * 128:LO_Z + (b + 1) * 128, i * TT + col0:i * TT + col0 + TT], [b_z])
                kb.act(z[:, :], z[:, :], AF.Silu, [b_z], [b_z])
                kb.tt("dve", yg[:, b, :], y[:, :], z[:, :], ALU.mult, [b_y, b_z], [b_yg])
                kb.tt("pool", sq[:, :], yg[:, b, :], yg[:, b, :], ALU.mult, [b_yg], [b_sq])
                kb.mm(ps[:, :TT], ONES, sq[:, :], b == 0, b == 3, [b_sq], [b_ps])
            kb.act(rs[:, :], ps[:, :TT], AF.Sqrt, [b_ps], [b_rs], bias=RMS_EPS, scale=1.0 / 512)
            P.op("dve", lambda i: nc.vector.reciprocal(out=rs[:, :], in_=rs[:, :]), [b_rs], [b_rs])
            for b in range(4):
                kb.stt("dve", ob[:, :], yg[:, b, :], dsk[:, b, 0:1], rs[:, :], ALU.mult, ALU.mult, [b_yg, b_dsk, b_rs], [b_ob])
                kb.stq("sp", lambda i, b=b: BR3[b * 128:(b + 1) * 128, i * TT + tok0:i * TT + tok0 + TT], ob[:, :], [b_ob])
            P.emit(T)


ALU_ALPHA = ALPHA


class HRes:
    def __init__(self, hc, hq, seg, cc):
        self.hc, self.hq, self.seg, self.cc = hc, hq, seg, cc

    def __getitem__(self, key):
        rs, cs = key
        if self.seg == 0:
            return self.hc[rs, cs]
        return self.hq[rs.start // 128, self.cc, :, cs]


def own_segs(env, with_ctx):
    CWH, NCC = env["CWH"], env["NCC"]
    res = []
    if with_ctx:
        res.append((0, 0, 0, CTX, 0))
    for cc in range(NCC):
        res.append((1, cc, CTX + cc * CWH, CWH, 1))
    return res


def merge_phase(env, l, PTmg, BRgl, BRgx, w_br, w_out, hc, hq, pv_ln, with_ctx):
    nc, P, kb, S, ada = env["nc"], env["P"], env["kb"], env["S"], env["ada"]
    TQ, CWH = env["TQ"], env["CWH"]
    for (seg, cc, mcol0, ntok, cj) in own_segs(env, with_ctx):
        TT = min(512, ntok)
        T = ntok // TT
        hres = HRes(hc, hq, seg, cc)
        with contextlib.ExitStack() as st:
            pln, b_pln = kb.sb(st, "pln", [128, 64])
            bt, b_bt = kb.sb(st, "bt", [128, NCH, TT], BF16)
            wt = [kb.sb(st, "wtm", [128, NCH, 512], BF16) for _ in range(2)]
            m, b_m = kb.sb(st, "mm_", [128, NCH, TT])
            mb, b_mb = kb.sb(st, "mb", [128, NCH, TT], BF16)
            r, b_r = kb.sb(st, "r", [128, NCH, TT])
            gt = [kb.sb(st, "gtm", [128, TT]) for _ in range(2)]
            tmp, b_tmp = kb.sb(st, "tmpm", [128, TT])
            ps = [kb.psb(st, "psm_", [128, 512]) for _ in range(4)]
            kb.ld("sp", pln[:, :], lambda i: pv_ln[l], [b_pln])

            def tokfn(i):
                return i * TT

            wi = 0
            n_ = 0
            for j in range(4):
                if seg == 0:
                    kb.ld("sp", bt[:, :, :], lambda i, j=j: BRgx[j].rearrange("r (b p) t -> p (r b) t", p=128), [b_bt])
                else:
                    for rb in range(4):
                        kb.ld("sp", bt[:, rb:NCH:4, :], lambda i, j=j, rb=rb: BRgl[j][rb, 0, :, :, cc * CWH + i * TT:cc * CWH + i * TT + TT]
                              .rearrange("r p t -> p r t"), [b_bt], cj=NQ * 128 * TQ)
                for og in range(4):
                    wt_, b_wt = wt[wi % 2]
                    wi += 1
                    kb.ld("pool", wt_[:, :, :], lambda i, j=j, og=og: w_br[l, j, :, og * 512:(og + 1) * 512].rearrange("(c p) n -> p c n", p=128), [b_wt])
                    for m_ in range(4):
                        ob = og * 4 + m_
                        ps_, b_ps = ps[n_ % 4]
                        gt_, b_gt = gt[n_ % 2]
                        n_ += 1
                        for c in range(NCH):
                            kb.mm(ps_[:, :TT], wt_[:, c, m_ * 128:(m_ + 1) * 128], bt[:, c, :], c == 0, c == NCH - 1, [b_wt, b_bt], [b_ps])
                        row = j * D + ob * 128
                        kb.ld("sp", gt_[:, :], lambda i, row=row: PTmg[row:row + 128, mcol0 + i * TT:mcol0 + i * TT + TT], [b_gt])
                        kb.act(gt_[:, :], gt_[:, :], AF.Sigmoid, [b_gt], [b_gt])
                        if j == 0:
                            kb.tt("dve", m[:, ob, :], ps_[:, :TT], gt_[:, :], ALU.mult, [b_ps, b_gt], [b_m])
                        else:
                            kb.tt("dve", tmp[:, :], ps_[:, :TT], gt_[:, :], ALU.mult, [b_ps, b_gt], [b_tmp])
                            kb.tt("pool", m[:, ob, :], m[:, ob, :], tmp[:, :], ALU.add, [b_tmp, b_m], [b_m])
            kb.cp("act", mb[:, :, :], m[:, :, :], [b_m], [b_mb])
            for og in range(4):
                wt_, b_wt = wt[wi % 2]
                wi += 1
                kb.ld("pool", wt_[:, :, :], lambda i, og=og: w_out[l, :, og * 512:(og + 1) * 512].rearrange("(c p) n -> p c n", p=128), [b_wt])
                for m_ in range(4):
                    ob = og * 4 + m_
                    ps_, b_ps = ps[n_ % 4]
                    gt_, b_gt = gt[n_ % 2]
                    n_ += 1
                    for c in range(NCH):
                        kb.mm(ps_[:, :TT], wt_[:, c, m_ * 128:(m_ + 1) * 128], mb[:, c, :], c == 0, c == NCH - 1, [b_wt, b_mb], [b_ps])
                    kb.ld("sp", gt_[:, :], lambda i, ob=ob: hres[ob * 128:(ob + 1) * 128, tokfn(i):tokfn(i) + TT], [b_gt])
                    kb.ts("dve", tmp[:, :], ps_[:, :TT], ada(l, seg, 2, ob), None, ALU.mult, None, [b_ps], [b_tmp])
                    kb.stt("dve", r[:, ob, :], gt_[:, :], ALU_ALPHA, tmp[:, :], ALU.mult, ALU.add, [b_gt, b_tmp], [b_r])
            ln_tail(env, st, l, 0, r, b_r, TT, hres, tokfn, pln, b_pln)
            P.emit(T)


def ffn_act(env, l, UT, pv_ffn, ACTc, with_ctx):
    nc, P, kb, S = env["nc"], env["P"], env["kb"], env["S"]
    TT = min(1024, env["TQ"])
    tiles = seg_tiles(env, TT)
    if not with_ctx:
        tiles = tiles[1:]
    pvv = pv_ffn[l].rearrange("b p k -> (b p) k")
    with contextlib.ExitStack() as st:
        pvg, b_pvg = kb.sb(st, "pvg", [128, 4])
        pvv_, b_pvv = kb.sb(st, "pvv", [128, 4])
        gh, b_gh = kb.sb(st, "gh", [128, TT + 2])
        vh, b_vh = kb.sb(st, "vh", [128, TT + 2])
        gc, b_gc = kb.sb(st, "gc", [128, TT])
        vc, b_vc = kb.sb(st, "vc", [128, TT])
        ob, b_ob = kb.sb(st, "obf", [128, TT], BF16)
        kb.ld("sp", pvg[:, :], lambda i: pvv[i * 128:(i + 1) * 128, :], [b_pvg])
        kb.ld("sp", pvv_[:, :], lambda i: pvv[NBF * 128 + i * 128:NBF * 128 + (i + 1) * 128, :], [b_pvv])
        for sg in tiles:
            for (seg, col, tok, tn) in sg:
                kb.ld("sp", gh[:, :tn + 2], lambda i, col=col, tn=tn: UT[i * 128:(i + 1) * 128, col - 1:col + tn + 1], [b_gh])
                kb.ld("sp", vh[:, :tn + 2], lambda i, col=col, tn=tn: UT[DFQ + i * 128:DFQ + (i + 1) * 128, col - 1:col + tn + 1], [b_vh])
                for (src, b_src, dst, b_dst, pv, b_pv) in ((gh, b_gh, gc, b_gc, pvg, b_pvg), (vh, b_vh, vc, b_vc, pvv_, b_pvv)):
                    kb.ts("dve", dst[:, :tn], src[:, 0:tn], pv[:, 0:1], pv[:, 3:4], ALU.mult, ALU.add, [b_src, b_pv], [b_dst])
                    for k in (1, 2):
                        kb.stt("dve", dst[:, :tn], src[:, k:k + tn], pv[:, k:k + 1], dst[:, :tn], ALU.mult, ALU.add,
                               [b_src, b_pv, b_dst], [b_dst])
                kb.act(gc[:, :tn], gc[:, :tn], AF.Silu, [b_gc], [b_gc])
                kb.tt("pool", ob[:, :tn], gc[:, :tn], vc[:, :tn], ALU.mult, [b_gc, b_vc], [b_ob])
                kb.stq("sp", lambda i, tok=tok, tn=tn: ACTc[i * 128:(i + 1) * 128, tok:tok + tn], ob[:, :tn], [b_ob])
        P.emit(NBF)


def ffn_down_phase(env, l, ACgl, ACgx, ffn_down, hc, hq, pv_ln, with_ctx):
    nc, P, kb, S, ada = env["nc"], env["P"], env["kb"], env["S"], env["ada"]
    NB = DFF // 128
    TQ, CWH = env["TQ"], env["CWH"]
    for (seg, cc, mcol0, ntok, cj) in own_segs(env, with_ctx):
        TT = min(512, ntok)
        T = ntok // TT
        hres = HRes(hc, hq, seg, cc)
        with contextlib.ExitStack() as st:
            pln, b_pln = kb.sb(st, "plnf", [128, 64])
            actb, b_actb = kb.sb(st, "actb", [128, NB, TT], BF16)
            wt = [kb.sb(st, "wtf", [128, NB, 256], BF16) for _ in range(2)]
            r, b_r = kb.sb(st, "rf", [128, NCH, TT])
            ht = [kb.sb(st, "htf", [128, TT]) for _ in range(2)]
            tmp, b_tmp = kb.sb(st, "tmpf", [128, TT])
            ps = [kb.psb(st, "psf", [128, 512]) for _ in range(4)]
            kb.ld("sp", pln[:, :], lambda i: pv_ln[l], [b_pln])

            def tokfn(i):
                return i * TT

            if seg == 0:
                kb.ld("sp", actb[:, :, :], lambda i: ACgx.rearrange("r (b p) t -> p (r b) t", p=128), [b_actb])
            else:
                for rb in range(NBF):
                    kb.ld("sp", actb[:, rb:NB:NBF, :], lambda i, rb=rb: ACgl[rb, 0, :, :, cc * CWH + i * TT:cc * CWH + i * TT + TT]
                          .rearrange("r p t -> p r t"), [b_actb], cj=NQ * 128 * TQ)
            n_ = 0
            for og in range(8):
                wt_, b_wt = wt[og % 2]
                kb.ld("pool", wt_[:, :, :], lambda i, og=og: ffn_down[l, :, og * 256:(og + 1) * 256].rearrange("(c p) n -> p c n", p=128), [b_wt])
                for m_ in range(2):
                    ob = og * 2 + m_
                    ps_, b_ps = ps[n_ % 4]
                    ht_, b_ht = ht[n_ % 2]
                    n_ += 1
                    for c in range(NB):
                        kb.mm(ps_[:, :TT], wt_[:, c, m_ * 128:(m_ + 1) * 128], actb[:, c, :], c == 0, c == NB - 1, [b_wt, b_actb], [b_ps])
                    kb.ld("sp", ht_[:, :], lambda i, ob=ob: hres[ob * 128:(ob + 1) * 128, tokfn(i):tokfn(i) + TT], [b_ht])
                    kb.ts("dve", tmp[:, :], ps_[:, :TT], ada(l, seg, 5, ob), None, ALU.mult, None, [b_ps], [b_tmp])
                    kb.stt("dve", r[:, ob, :], ht_[:, :], ALU_ALPHA, tmp[:, :], ALU.mult, ALU.add, [b_ht, b_tmp], [b_r])
            ln_tail(env, st, l, 1, r, b_r, TT, hres, tokfn, pln, b_pln)
            P.emit(T)


def make_consts(S):
    idx = np.arange(128)
    t_, l_ = idx[:, None], idx[None, :]
    c = np.zeros((128, 10, 128), np.float32)
    c[:, 0, :] = np.eye(128)
    partner = np.where((idx % 64) < 32, idx + 32, idx - 32)
    R = np.zeros((128, 128), np.float32)
    R[idx, partner] = 1.0
    c[:, 1, :] = R
    c[:, 2, :] = 1.0
    c[:, 3, :] = (t_ <= l_)
    c[:, 4, :] = (t_ > l_)
    c[:, 5, :] = np.where(l_ < t_, -30000.0, 0.0)
    c[:, 6, :] = (t_ >= l_)
    c[:, 7, :] = (t_ < l_)
    c[:, 8, :] = np.where(l_ > t_, -30000.0, 0.0)
    tok = np.arange(S)
    r = (tok // 64).astype(np.float32)
    cc = (tok % 64).astype(np.float32)
    inv = (np.float32(10000.0) ** (-np.arange(32, dtype=np.float32) / np.float32(32))).astype(np.float32)
    ang = np.zeros((128, S), np.float32)
    for d in range(128):
        pos = r if d < 64 else cc
        ang[d] = pos * inv[d % 32]
    sgn = np.where((idx % 64) < 32, -1.0, 1.0).astype(np.float32)
    rope = np.stack([np.cos(ang), np.sin(ang) * sgn[:, None]]).astype(np.float32)
    return c, rope


def pack_shared(inp, depth):
    L = depth
    f = lambda a: np.ascontiguousarray(a, dtype=np.float32)
    pv_ln = np.zeros((L, 128, 64), np.float32)
    pv_ln[:, :, 0:32] = inp["ln_g"][:L].reshape(L, 2, 16, 128).transpose(0, 3, 1, 2).reshape(L, 128, 32)
    pv_ln[:, :, 32:64] = inp["ln_b"][:L].reshape(L, 2, 16, 128).transpose(0, 3, 1, 2).reshape(L, 128, 32)
    return {
        "w_ada": f(inp["w_ada"][:L]),
        "w_mg": f(inp["w_in"][:L][..., 18496:26688]),
        "pv_att": f(np.stack([inp["attn_q_norm"][:L], inp["attn_k_norm"][:L]], -1)),
        "pv_ln": pv_ln,
        "w_br": f(np.stack([inp["w_br_lru"][:L], inp["w_br_sconv"][:L], inp["w_br_attn"][:L],
                            inp["w_br_ssm"][:L]], 1)),
        "w_out": f(inp["w_out"][:L]), "ffn_down": f(inp["ffn_down"][:L]),
    }


def pack_quarter(inp, depth, j):
    L = depth
    f = lambda a: np.ascontiguousarray(a, dtype=np.float32)
    w = inp["w_in"][:L]
    q5 = slice(j * 512, (j + 1) * 512)
    dtc = np.concatenate([np.arange(18432 + d * 32 + 8 * j, 18432 + d * 32 + 8 * j + 8) for d in range(2)])
    cols = np.concatenate([np.arange(o + j * 512, o + (j + 1) * 512) for o in (0, 2048, 4096, 6144, 8192, 10240)]
                          + [np.arange(12288 + j * 128, 12288 + (j + 1) * 128), np.arange(12800 + j * 128, 12800 + (j + 1) * 128),
                             np.arange(13312 + j * 512, 13312 + (j + 1) * 512), np.arange(15360 + j * 512, 15360 + (j + 1) * 512),
                             np.arange(17408 + j * 128, 17408 + (j + 1) * 128), np.arange(17920 + j * 128, 17920 + (j + 1) * 128), dtc])
    assert cols.shape[0] == NCOLC
    bl = slice(4 * j, 4 * j + 4)
    pv_lru = np.zeros((L, 4, 128, 12), np.float32)
    pv_lru[..., 0:4] = inp["lru_conv_w"][:L].reshape(L, 4, 16, 128).transpose(0, 2, 3, 1)[:, bl]
    pv_lru[..., 4] = inp["lru_conv_b"][:L].reshape(L, 16, 128)[:, bl]
    pv_lru[..., 5:9] = inp["lru_gate_b"][:L].reshape(L, 4, 16, 128).transpose(0, 2, 3, 1)[:, bl]
    pv_lru[..., 9:11] = inp["lru_lambda"][:L].reshape(L, 2, 16, 128).transpose(0, 2, 3, 1)[:, bl]
    gw = inp["lru_gate_w"][:L].reshape(L, 4, 16, 128, 128)[:, :, bl].reshape(L, 4 * 4 * 128, 128)
    pv_sc = np.zeros((L, 4, 128, 4), np.float32)
    pv_sc[..., 0:3] = inp["sconv_w"][:L].reshape(L, 3, 16, 128).transpose(0, 2, 3, 1)[:, bl]
    sblk = [4 * j, 4 * j + 1, 4 * j + 2, 4 * j + 3, 16 + j, 20 + j]
    pv_ssc = np.zeros((L, 6, 128, 8), np.float32)
    pv_ssc[..., 0:4] = inp["ssm_conv_w"][:L].reshape(L, 4, 24, 128).transpose(0, 2, 3, 1)[:, sblk]
    pv_ssc[..., 4] = inp["ssm_conv_b"][:L].reshape(L, 24, 128)[:, sblk]
    pv_ssd = np.zeros((L, 128, 4, 2), np.float32)
    pv_ssd[..., 0] = inp["ssm_norm"][:L].reshape(L, 16, 128)[:, bl].transpose(0, 2, 1)
    pv_ssd[..., 1] = np.repeat(inp["ssm_d"][:L], 64, axis=1).reshape(L, 16, 128)[:, bl].transpose(0, 2, 1)
    pv_dt = np.zeros((L, 128, 32), np.float32)
    pv_dt[:, :, 0:16] = inp["ssm_dt_bias"][:L][:, :, 8 * j:8 * j + 8].reshape(L, 1, 16)
    pv_dt[:, :, 16:32] = inp["ssm_a_log"][:L][:, :, 8 * j:8 * j + 8].reshape(L, 1, 16)
    fb = list(range(NBF * j, NBF * j + NBF)) + list(range(44 + NBF * j, 44 + NBF * j + NBF))
    pv_ffn = np.zeros((L, 2 * NBF, 128, 4), np.float32)
    pv_ffn[..., 0:3] = inp["ffn_conv_w"][:L].reshape(L, 3, 88, 128).transpose(0, 2, 3, 1)[:, fb]
    pv_ffn[..., 3] = inp["ffn_conv_b"][:L].reshape(L, 88, 128)[:, fb]
    up = inp["ffn_up"][:L]
    return {
        "w_inc": f(w[..., cols]), "pv_lru": pv_lru, "lru_gw": f(gw), "pv_sc": pv_sc, "pv_ssc": pv_ssc,
        "pv_ssd": pv_ssd, "pv_dt": pv_dt, "pv_ffn": pv_ffn,
        "ffn_upc": f(np.concatenate([up[..., j * DFQ:(j + 1) * DFQ], up[..., DFF + j * DFQ:DFF + (j + 1) * DFQ]], -1)),
        "cid": np.array([[j, 0, 0, 0]], np.int32),
    }


_NC_CACHE = {}
_CWH = [2048]


def run(cfg, ins_per_core):
    key = tuple(sorted(cfg.items()))
    if key not in _NC_CACHE:
        _NC_CACHE[key] = build(cfg)
    nc = _NC_CACHE[key]
    n = len(ins_per_core)
    return run_bass_kernel_spmd(nc, ins_per_core, core_ids=list(range(n)))


def kernel(**inp):
    inp = {k: np.asarray(v) for k, v in inp.items()}
    B, S, _ = inp["x"].shape
    depth = inp["w_in"].shape[0]
    TQ = S // NQ
    cfg = {"S": S, "depth": depth, "cwh": _CWH[0]}
    shared = pack_shared(inp, depth)
    shared["consts"], shared["ropeT"] = make_consts(S)
    quarters = [pack_quarter(inp, depth, j) for j in range(NQ)]
    ins = []
    for b in range(2):
        bb = min(b, B - 1)
        for j in range(NQ):
            m = dict(shared)
            m.update(quarters[j])
            m["xc"] = np.ascontiguousarray(inp["ctx"][bb].T.astype(np.float32))
            CWH = min(_CWH[0], TQ)
            xqt = inp["x"][bb][j * TQ:(j + 1) * TQ].T.astype(np.float32)
            m["xq"] = np.ascontiguousarray(xqt.reshape(NCH, 128, TQ // CWH, CWH).transpose(0, 2, 1, 3))
            m["mod"] = np.ascontiguousarray(np.stack([inp["c_ctx"], inp["c"][bb]]).astype(np.float32))
            ins.append(m)
    res = run(cfg, ins)
    out = np.zeros((B, S, D), np.float32)
    for b in range(B):
        for j in range(NQ):
            o = res.results[b * NQ + j]["out"]
            out[b, j * TQ:(j + 1) * TQ] = o.transpose(0, 2, 1, 3).reshape(D, TQ).T
    return out
```

```python
import contextlib
import re
import math
import numpy as np
import concourse.bass as bass
import concourse.mybir as mybir
from concourse.bass_utils import run_bass_kernel_spmd

F32 = mybir.dt.float32
BF16 = mybir.dt.bfloat16
AF = mybir.ActivationFunctionType
ALU = mybir.AluOpType

D = 2048
NCH = 16
CTX = 256
IN_W = 26688
DFF = 5632
ALPHA = 8.0 ** 0.25
LN_EPS = 1e-5
RMS_EPS = 1e-6
O_AX, O_AG, O_SB, O_SG, O_SX, O_Q, O_K, O_V, O_Z, O_XBC, O_MG, O_DT = (
    0, 2048, 4096, 6144, 8192, 10240, 12288, 12800, 13312, 15360, 18432, 26624)
W_IN_DT0, W_IN_MG0 = 18432, 18496


class Split:
    def __init__(self, pieces):
        self.pieces = pieces

    def __getitem__(self, key):
        rs, cs = key
        for (a, b, ap) in self.pieces:
            if a <= rs.start and rs.stop <= b:
                return ap[rs.start - a:rs.stop - a, cs]
        raise ValueError("row range %s crosses scratch pieces" % (rs,))


class Stack3:
    def __init__(self, aps):
        self.aps = aps

    def __getitem__(self, key):
        j, rs, cs = key
        return self.aps[j][rs, cs]


_UID = [0]


def U(name):
    _UID[0] += 1
    return "%s_%d" % (name, _UID[0])


class Buf:
    __slots__ = ("name",)

    def __init__(self, name):
        self.name = name


def dsl(start, size):
    if isinstance(start, int):
        return slice(start, start + size)
    return bass.ds(start, size)


class Prog:
    NS = 12

    def __init__(self, nc, stack):
        self.nc = nc
        self.E = {"pe": nc.tensor, "act": nc.scalar, "dve": nc.vector, "pool": nc.gpsimd, "sp": nc.sync}
        self.sem = {}
        self.base = {}
        for e in self.E:
            self.sem[e] = stack.enter_context(nc.semaphore("s_" + e))
            self.base[e] = 0
        self.dsems = {}
        for q in ("sp", "pool", "act"):
            self.dsems[q] = []
            for k in range(self.NS):
                nm = "d_%s%d" % (q, k)
                self.sem[nm] = stack.enter_context(nc.semaphore(nm))
                self.base[nm] = 0
                self.dsems[q].append(nm)
        self.rt = {e: self.E[e].alloc_register("rt_" + e) for e in self.E}
        self.rt2 = {e: self.E[e].alloc_register("rtd_" + e) for e in self.E}
        self.loopregs = nc.alloc_registers("loop_i", engines=mybir.ALL_ENGINES)
        self.ET = {"pe": mybir.EngineType.PE, "act": mybir.EngineType.Activation, "dve": mybir.EngineType.DVE,
                   "pool": mybir.EngineType.Pool, "sp": mybir.EngineType.SP}
        self.sem["cc"] = stack.enter_context(nc.semaphore("s_cc"))
        self.base["cc"] = 0
        self.rt3 = {e: self.E[e].alloc_register("rtc_" + e) for e in ("sp", "pool", "act")}
        self.rcore = {e: self.E[e].alloc_register("rcore_" + e) for e in ("sp", "pool", "act")}
        self.ops = []
        self.nbuf = 0

    def buf(self, name=None):
        self.nbuf += 1
        return Buf(name or "b%d" % self.nbuf)

    def op(self, eng, fn, r=(), w=()):
        self.ops.append((eng, fn, tuple(r), tuple(w), False))

    def dma(self, q, fn, r=(), w=()):
        self.ops.append((q, fn, tuple(r), tuple(w), True))

    def load_core_id(self, cid_ap):
        for e in ("sp", "pool", "act"):
            self.E[e].reg_load(self.rcore[e], cid_ap)

    def coll(self, in_ap, out_ap, groups, r=(), w=()):
        def fn(idx):
            return self.nc.gpsimd.collective_compute("AllGather", mybir.AluOpType.bypass, replica_groups=groups,
                                                     ins=[in_ap.opt()], outs=[out_ap.opt()])
        self.ops.append(("pool", fn, tuple(r), tuple(w), 2))

    def dma2(self, q, out_fn, in_fn, r=(), w=(), cj=(0, 0)):
        def fn(idx):
            e = self.E[q]
            if isinstance(idx, int) and cj == (0, 0):
                return e.dma_start(out=out_fn(idx), in_=in_fn(idx))
            aps = []
            for f, c_ in zip((out_fn, in_fn), cj):
                if isinstance(idx, int):
                    a0 = f(idx)
                    k = 0
                else:
                    a0, a1 = f(0), f(1)
                    k = a1.offset - a0.offset
                if k == 0 and c_ == 0:
                    aps.append(a0)
                    continue
                rt = self.rt2[q]
                if k != 0:
                    e.reg_mul(rt, idx[self.ET[q]], k)
                    e.reg_add(rt, rt, a0.offset)
                else:
                    e.reg_mov(rt, a0.offset)
                if c_ != 0:
                    e.reg_mul(self.rt3[q], self.rcore[q], c_)
                    e.reg_add(rt, rt, self.rt3[q])
                aps.append(bass.AP(a0.tensor, rt, [list(x) for x in a0.ap]))
            ins = e.dma_start(out=aps[0], in_=aps[1])
            m = re.search(r"R\[(\w+?)_tmp_(\d+)\]", ins.concise())
            if m:
                pre, tid = m.group(1), int(m.group(2))
                RH = type(self.rt2[q])
                for nm in ("%s_tmp_%d" % (pre, tid), "%s_%s_rtd_%s_snap_%d" % (pre, pre, q, tid - 2)):
                    e.free_register(RH(nm, self.rt2[q].engine))
            return ins
        self.ops.append((q, fn, tuple(r), tuple(w), True))

    def emit(self, T=1):
        ops = self.ops
        self.ops = []
        n = len(ops)
        lastw = {}
        readers = {}
        deps = [None] * n
        for j, (eng, fn, r, w, isd) in enumerate(ops):
            dj = set()
            for b in r:
                if b in lastw:
                    dj.add(lastw[b])
            for b in w:
                if b in lastw:
                    dj.add(lastw[b])
                for k in readers.get(b, ()):
                    dj.add(k)
            dj.discard(j)
            deps[j] = dj
            for b in r:
                readers.setdefault(b, []).append(j)
            for b in w:
                lastw[b] = j
                readers[b] = []
        signal = [False] * n
        for j in range(n):
            for d in deps[j]:
                if ops[d][0] == "pe" and ops[j][0] == "pe" and not ops[d][4] and not ops[j][4]:
                    continue
                signal[d] = True
        lastop = {}
        for j in range(n):
            if ops[j][4]:
                signal[j] = True
            else:
                lastop[ops[j][0]] = j
        for e, j in lastop.items():
            signal[j] = True
        cnt = {}
        sig = [None] * n
        slot_prev = {}
        pre_wait = [None] * n
        rr = {"sp": 0, "pool": 0, "act": 0}
        for j in range(n):
            eng, fn, r, w, isd = ops[j]
            if not signal[j]:
                continue
            if isd == 2:
                if "cc" in slot_prev:
                    pre_wait[j] = slot_prev["cc"]
                cnt["cc"] = cnt.get("cc", 0) + 1
                sig[j] = ("cc", cnt["cc"])
                slot_prev["cc"] = sig[j]
            elif isd:
                s = self.dsems[eng][rr[eng] % self.NS]
                rr[eng] += 1
                if s in slot_prev:
                    pre_wait[j] = slot_prev[s]
                cnt[s] = cnt.get(s, 0) + 16
                sig[j] = (s, cnt[s])
                slot_prev[s] = sig[j]
            else:
                cnt[eng] = cnt.get(eng, 0) + 1
                sig[j] = (eng, cnt[eng])
        nc = self.nc

        def body(idx):
            waited = {e: {} for e in self.E}

            def wait(e, s, c):
                if waited[e].get(s, 0) >= c:
                    return
                waited[e][s] = c
                if isinstance(idx, int):
                    self.E[e].wait_ge(self.sem[s], self.base[s] + idx * cnt[s] + c)
                else:
                    rt = self.rt[e]
                    self.E[e].reg_mul(rt, idx[self.ET[e]], cnt[s])
                    self.E[e].reg_add(rt, rt, self.base[s] + c)
                    self.E[e].wait_ge(self.sem[s], rt)

            for j in range(n):
                eng, fn, r, w, isd = ops[j]
                for d in sorted(deps[j]):
                    if sig[d] is None:
                        continue
                    if (not isd) and eng == "pe" and ops[d][0] == "pe" and not ops[d][4]:
                        continue
                    wait(eng, *sig[d])
                if pre_wait[j] is not None:
                    wait(eng, *pre_wait[j])
                ins = fn(idx)
                if sig[j] is not None:
                    if isd == 2:
                        ins.then_inc(self.sem["cc"])
                    else:
                        ins.then_inc(self.sem[sig[j][0]], 16 if isd else 1)
            for q in ("sp", "pool", "act"):
                for s in self.dsems[q]:
                    if s in cnt:
                        wait(q, s, cnt[s])
            if "cc" in cnt:
                wait("pool", "cc", cnt["cc"])
            for e, j in lastop.items():
                if e != "sp":
                    wait("sp", *sig[j])
            nc.all_engine_barrier()

        if T > 1:
            ENG = mybir.ALL_ENGINES
            regs = self.loopregs
            lid = nc.next_id()
            ls, le = "myl_%d_loop" % lid, "myl_%d_end" % lid
            nc.regs_mov(regs, 0)
            nc.br(ls, engines=ENG)
            with nc.body(ls, valid_engines=ENG):
                body(regs)
                nc.regs_alu(regs, regs, 1, op=mybir.AluOpType.add)
                nc.br_lt(regs, T, on_true=ls, on_false=le, engines=ENG)
            nc.switch_bb(le)
        else:
            body(0)
        for s, c in cnt.items():
            self.base[s] += T * c
        return n


class KB:
    def __init__(self, nc, P):
        self.nc, self.P = nc, P
        self.EN = {"dve": nc.vector, "pool": nc.gpsimd, "act": nc.scalar}

    def sb(self, st, name, shape, dt=F32):
        return st.enter_context(self.nc.sbuf_tensor(U(name), list(shape), dt)), self.P.buf()

    def psb(self, st, name, shape, dt=F32):
        return st.enter_context(self.nc.psum_tensor(U(name), list(shape), dt)), self.P.buf()

    def tt(self, eng, out, in0, in1, op, r, w):
        e = self.EN[eng]
        self.P.op(eng, lambda i: e.tensor_tensor(out=out, in0=in0, in1=in1, op=op), r, w)

    def ts(self, eng, out, in0, s1, s2, op0, op1, r, w):
        e = self.EN[eng]
        if op1 is None:
            self.P.op(eng, lambda i: e.tensor_scalar(out=out, in0=in0, scalar1=s1, scalar2=None, op0=op0), r, w)
        else:
            self.P.op(eng, lambda i: e.tensor_scalar(out=out, in0=in0, scalar1=s1, scalar2=s2, op0=op0, op1=op1), r, w)

    def stt(self, eng, out, in0, sc, in1, op0, op1, r, w):
        e = self.EN[eng]
        self.P.op(eng, lambda i: e.scalar_tensor_tensor(out=out, in0=in0, scalar=sc, in1=in1, op0=op0, op1=op1), r, w)

    def act(self, out, in_, func, r, w, bias=None, scale=None):
        kw = {}
        if bias is not None:
            kw["bias"] = bias
        if scale is not None:
            kw["scale"] = scale
        self.P.op("act", lambda i: self.nc.scalar.activation(out=out, in_=in_, func=func, **kw), r, w)

    def cp(self, eng, out, in_, r, w):
        if eng == "act":
            self.P.op("act", lambda i: self.nc.scalar.copy(out=out, in_=in_), r, w)
        else:
            e = self.EN[eng]
            self.P.op(eng, lambda i: e.tensor_copy(out=out, in_=in_), r, w)

    def ms(self, eng, ap, val, w):
        e = self.EN[eng]
        self.P.op(eng, lambda i: e.memset(ap, val), (), w)

    def mm(self, out, lhsT, rhs, start, stop, r, w):
        self.P.op("pe", lambda i: self.nc.tensor.matmul(out, lhsT=lhsT, rhs=rhs, start=start, stop=stop), r, w)

    def tr(self, out, in_, ident, r, w):
        self.P.op("pe", lambda i: self.nc.tensor.transpose(out, in_, ident), r, w)

    def ld(self, q, out, in_fn, w, r=(), cj=0):
        self.P.dma2(q, lambda i: out, in_fn, r, w, cj=(0, cj))

    def stq(self, q, out_fn, in_, r, w=(), cj=0):
        self.P.dma2(q, out_fn, lambda i: in_, r, w, cj=(cj, 0))


def bc_last(ap2, m):
    a = [list(x) for x in ap2.ap]
    return bass.AP(ap2.tensor, ap2.offset, a + [[0, m]])


C0 = 2


def seg_tiles(env, TT):
    res = []
    for (seg, col0, tok0, ntok) in env["SEGS"]:
        tn = min(TT, ntok)
        lst = [(seg, col0 + t, tok0 + t, tn) for t in range(0, ntok, tn)]
        res.append(lst)
    return res


def ln_tail(env, st, l, which, r, b_r, TT, hT, tokfn, pln, b_pln):
    nc, P, kb, cst = env["nc"], env["P"], env["kb"], env["cst"]
    ONES = cst[:, 2, :]
    psA, b_psA = kb.psb(st, "psA", [128, 512])
    psB, b_psB = kb.psb(st, "psB", [128, 512])
    sq, b_sq = kb.sb(st, "lsq", [128, TT])
    mu, b_mu = kb.sb(st, "lmu", [128, TT])
    rstd, b_rstd = kb.sb(st, "lrstd", [128, TT])
    ho = [kb.sb(st, "lho", [128, TT]) for _ in range(2)]
    for ob in range(NCH):
        kb.mm(psA[:, :TT], ONES, r[:, ob, :], ob == 0, ob == NCH - 1, [b_r], [b_psA])
        kb.tt("pool", sq[:, :], r[:, ob, :], r[:, ob, :], ALU.mult, [b_r], [b_sq])
        kb.mm(psB[:, :TT], ONES, sq[:, :], ob == 0, ob == NCH - 1, [b_sq], [b_psB])
    kb.act(mu[:, :], psA[:, :TT], AF.Copy, [b_psA], [b_mu], scale=1.0 / D)
    kb.tt("pool", sq[:, :], mu[:, :], mu[:, :], ALU.mult, [b_mu], [b_sq])
    kb.stt("dve", rstd[:, :], psB[:, :TT], 1.0 / D, sq[:, :], ALU.mult, ALU.subtract, [b_psB, b_sq], [b_rstd])
    kb.act(rstd[:, :], rstd[:, :], AF.Sqrt, [b_rstd], [b_rstd], bias=LN_EPS)
    P.op("dve", lambda i: nc.vector.reciprocal(out=rstd[:, :], in_=rstd[:, :]), [b_rstd], [b_rstd])
    for ob in range(NCH):
        ho_, b_ho = ho[ob % 2]
        kb.tt("pool", ho_[:, :], r[:, ob, :], mu[:, :], ALU.subtract, [b_r, b_mu], [b_ho])
        kb.tt("dve", ho_[:, :], ho_[:, :], rstd[:, :], ALU.mult, [b_ho, b_rstd], [b_ho])
        gcol = which * 16 + ob
        kb.ts("dve", ho_[:, :], ho_[:, :], pln[:, gcol:gcol + 1], pln[:, 32 + gcol:33 + gcol], ALU.mult, ALU.add,
              [b_ho, b_pln], [b_ho])
        kb.stq("sp", lambda i, ob=ob: hT[ob * 128:(ob + 1) * 128, tokfn(i):tokfn(i) + TT], ho_[:, :], [b_ho])


NQ = 4
LO_AX, LO_AG, LO_SB, LO_SG, LO_SX, LO_Q, LO_K, LO_V, LO_Z, LO_XBC, LO_DT = (
    0, 512, 1024, 1536, 2048, 2560, 3072, 3200, 3328, 3840, 4608)
NCOLC = 4624
DFQ = DFF // NQ
NBF = DFQ // 128
GROUPS = [[0, 1, 2, 3], [4, 5, 6, 7]]


def build(cfg):
    S = cfg["S"]
    depth = cfg["depth"]
    ST = CTX + S
    TQ = S // NQ
    L0 = C0 + CTX + 3
    WT = L0 + S + 2
    SEGS = ((0, C0, 0, CTX), (1, L0, CTX, S))
    nc = bass.Bass("TRN2", target_bir_lowering=False)
    stack = contextlib.ExitStack()
    with stack:
        P = Prog(nc, stack)
        kb = KB(nc, P)

        def din(name, shape, dt=F32):
            return nc.dram_tensor(name, list(shape), dt, kind="ExternalInput").ap()

        def dscr(name, shape, dt=F32):
            return nc.dram_tensor(name, list(shape), dt).ap()

        xc = din("xc", [D, CTX])
        CWH = min(cfg.get("cwh", 2048), TQ)
        NCC = TQ // CWH
        xq = din("xq", [NCH, NCC, 128, CWH])
        cid = din("cid", [1, 4], mybir.dt.int32)
        mod = din("mod", [2, D])
        w_ada = din("w_ada", [depth, D, 6 * D])
        w_inc = din("w_inc", [depth, D, NCOLC])
        w_mg = din("w_mg", [depth, D, 4 * D])
        consts = din("consts", [128, 10, 128])
        ropeT = din("ropeT", [2, 128, S])
        pv_lru = din("pv_lru", [depth, 4, 128, 12])
        lru_gw = din("lru_gw", [depth, 4 * 4 * 128, 128])
        pv_sc = din("pv_sc", [depth, 4, 128, 4])
        pv_att = din("pv_att", [depth, 128, 2])
        pv_ssc = din("pv_ssc", [depth, 6, 128, 8])
        pv_ssd = din("pv_ssd", [depth, 128, 4, 2])
        pv_dt = din("pv_dt", [depth, 128, 32])
        pv_ln = din("pv_ln", [depth, 128, 64])
        pv_ffn = din("pv_ffn", [depth, 2 * NBF, 128, 4])
        w_br = din("w_br", [depth, 4, D, D])
        w_out = din("w_out", [depth, D, D])
        ffn_upc = din("ffn_upc", [depth, D, 2 * DFQ])
        ffn_down = din("ffn_down", [depth, DFF, D])
        out = nc.dram_tensor("out", [NCH, NCC, 128, CWH], F32, kind="ExternalOutput").ap()
        PT = Split([(0, 2560, dscr("PTa", [2560, WT])), (2560, 4736, dscr("PTb", [4736 - 2560, WT]))])
        PTmg = dscr("PTmg", [4 * D, CTX + TQ])
        UT = Split([(0, 2 * DFQ, dscr("UTc", [2 * DFQ, WT]))])
        U3 = dscr("U3", [768, ST])
        YS = dscr("YS", [512, ST])
        BRl = [dscr("BRl%d" % j, [4, NQ, 128, TQ], BF16) for j in range(4)]
        BRx = [dscr("BRx%d" % j, [512, CTX], BF16) for j in range(4)]
        BRgl = [dscr("BRgl%d" % j, [4, NQ, NQ, 128, TQ], BF16) for j in range(4)]
        BRgx = [dscr("BRgx%d" % j, [NQ, 512, CTX], BF16) for j in range(4)]
        ACl = dscr("ACl", [NBF, NQ, 128, TQ], BF16)
        ACx = dscr("ACx", [DFQ, CTX], BF16)
        ACgl = dscr("ACgl", [NBF, NQ, NQ, 128, TQ], BF16)
        ACgx = dscr("ACgx", [NQ, DFQ, CTX], BF16)

        class BRW:
            def __init__(self, lat, ctxt):
                self.lat, self.ctxt = lat, ctxt

            def __getitem__(self, key):
                rs, cs = key
                blk = rs.start // 128
                assert rs.stop - rs.start == 128 and rs.start % 128 == 0
                if cs.start < CTX:
                    return self.ctxt[rs, cs]
                t = cs.start - CTX
                q, off = t // TQ, t % TQ
                assert off + (cs.stop - cs.start) <= TQ
                return self.lat[blk, q, :, off:off + cs.stop - cs.start]

        BRc = [BRW(BRl[j], BRx[j]) for j in range(4)]
        ACTc = BRW(ACl, ACx)
        QTg = dscr("QTg", [128, 4 * S], BF16)
        QTc = dscr("QTc", [4, 128, CTX], BF16)
        KT = dscr("KT", [128, ST], BF16)
        VTM = dscr("VTM", [ST, 128], BF16)
        hc = dscr("hc", [D, CTX])
        hq = dscr("hq", [NCH, NCC, 128, CWH])
        Hg = dscr("Hg", [NCH, NCC, NQ, 128, CWH])

        P.load_core_id(cid[0:1, 0:1])
        cst, b_cst = kb.sb(stack, "cst", [128, 10, 128])
        cstb, b_cstb = kb.sb(stack, "cstb", [128, 10, 128], BF16)
        kb.ld("sp", cst[:, :, :], lambda i: consts, [b_cst])
        kb.cp("dve", cstb[:, :, :], cst[:, :, :], [b_cst], [b_cstb])
        with contextlib.ExitStack() as st:
            z, bz = kb.sb(st, "z", [128, 8])
            kb.ms("dve", z[:, :], 0.0, [bz])
            for dst in (PT, UT):
                for (pa, pb, pap) in dst.pieces:
                    for r0 in range(0, pb - pa, 128):
                        rn = min(128, pb - pa - r0)
                        for (c0, cw) in ((0, 2), (C0 + CTX, 3), (L0 + S, 2)):
                            kb.stq("sp", lambda i, pap=pap, r0=r0, rn=rn, c0=c0, cw=cw: pap[r0:r0 + rn, c0:c0 + cw],
                                   z[:rn, :cw], [bz])
            P.emit()

        adaT, b_ada = kb.sb(stack, "adaT", [128, depth * 2 * 96])
        with contextlib.ExitStack() as st:
            smod, b_smod = kb.sb(st, "smod", [128, 2 * NCH])
            smodb, b_smodb = kb.sb(st, "smodb", [128, 2 * NCH], BF16)
            sig_t, b_sig = kb.sb(st, "sig_t", [128, 2 * NCH])
            P.dma("sp", lambda i: nc.sync.dma_start(
                out=smod[:, :].rearrange("p (s k) -> p s k", s=2),
                in_=mod.rearrange("s (k p) -> p s k", p=128), allow_slow_non_contiguous=True), w=[b_smod])
            kb.act(sig_t[:, :], smod[:, :], AF.Sigmoid, [b_smod], [b_sig])
            kb.tt("dve", smodb[:, :], smod[:, :], sig_t[:, :], ALU.mult, [b_smod, b_sig], [b_smodb])
            P.emit()
            wt = [kb.sb(st, "adw", [128, NCH, 512], BF16) for k in range(2)]
            ps = [kb.psb(st, "adps", [128, 2]) for k in range(2)]
            smv = smodb[:, :].rearrange("p (s k) -> p k s", s=2)
            adv = adaT[:, :].rearrange("p (l s k) -> p l k s", l=depth, s=2)
            for l in range(depth):
                for cb in range(6 * D // 512):
                    k = cb % 2
                    kb.ld("pool", wt[k][0][:, :, :], lambda i, l=l, cb=cb: w_ada[l, :, cb * 512:(cb + 1) * 512]
                          .rearrange("(c p) n -> p c n", p=128), [wt[k][1]])
                    for m in range(4):
                        pk = (cb * 4 + m) % 2
                        for c in range(NCH):
                            kb.mm(ps[pk][0][:, :], wt[k][0][:, c, m * 128:(m + 1) * 128], smv[:, c, :],
                                  c == 0, c == NCH - 1, [wt[k][1], b_smodb], [ps[pk][1]])
                        kb.cp("dve", adv[:, l, cb * 4 + m, :], ps[pk][0][:, :], [ps[pk][1]], [b_ada])
            P.emit()

        def ada(l, seg, which, k=None):
            col = (l * 2 + seg) * 96 + which * NCH
            if k is None:
                return adaT[:, col:col + NCH]
            return adaT[:, col + k:col + k + 1]

        with contextlib.ExitStack() as st:
            t, bt = kb.sb(st, "cp", [128, 2048])
            for k in range(NCH):
                kb.ld("sp", t[:, :CTX], lambda i, k=k: xc[k * 128:(k + 1) * 128, 0:CTX], [bt])
                kb.stq("sp", lambda i, k=k: hc[k * 128:(k + 1) * 128, 0:CTX], t[:, :CTX], [bt])
                for cc in range(NCC):
                    kb.ld("sp", t[:, :CWH], lambda i, k=k, cc=cc: xq[k, cc, :, :], [bt])
                    kb.stq("sp", lambda i, k=k, cc=cc: hq[k, cc, :, :], t[:, :CWH], [bt])
            P.emit()

        env = dict(nc=nc, P=P, kb=kb, S=S, ST=ST, TQ=TQ, L0=L0, WT=WT, SEGS=SEGS, ada=ada, cst=cst, cstb=cstb,
                   depth=depth, CWH=CWH, NCC=NCC)
        TTp = min(1024, CWH)
        all_src = [dict(seg=0, TT=CTX, T=1, src=lambda c, i: hc[c * 128:(c + 1) * 128, 0:CTX], dcol=lambda i: C0)]
        own_src = [dict(seg=0, TT=CTX, T=1, src=lambda c, i: hc[c * 128:(c + 1) * 128, 0:CTX], dcol=lambda i: 0)]
        for cc in range(NCC):
            for r in range(NQ):
                all_src.append(dict(seg=1, TT=TTp, T=CWH // TTp,
                                    src=lambda c, i, r=r, cc=cc: Hg[c, cc, r, :, i * TTp:(i + 1) * TTp],
                                    dcol=lambda i, r=r, cc=cc: L0 + r * TQ + cc * CWH + i * TTp))
            own_src.append(dict(seg=1, TT=TTp, T=CWH // TTp, src=lambda c, i, cc=cc: hq[c, cc, :, i * TTp:(i + 1) * TTp],
                                dcol=lambda i, cc=cc: CTX + cc * CWH + i * TTp))

        def gather_h():
            for c in range(NCH):
                for cc in range(NCC):
                    P.coll(hq[c, cc], Hg[c, cc], GROUPS)
            P.emit()

        for l in range(depth):
            with_ctx = l < depth - 1
            gather_h()
            proj_phase(env, l, w_inc[l], PT, NCOLC, 1, 0, all_src)
            proj_phase(env, l, w_mg[l], Split([(0, 4 * D, PTmg)]), 4 * D, 1, 0, own_src if with_ctx else own_src[1:])
            lru_phase(env, l, PT, pv_lru, lru_gw, BRc[0])
            sconv_phase(env, l, PT, pv_sc, BRc[1])
            attn_prep(env, l, PT, pv_att, ropeT, QTg, QTc, KT, VTM, with_ctx)
            attn_main(env, l, QTg, QTc, KT, VTM, BRc[2], with_ctx)
            ssd_conv(env, l, PT, pv_ssc, U3)
            ssd_main(env, l, PT, U3, pv_dt, pv_ssd, YS)
            ssd_norm(env, l, PT, YS, pv_ssd, BRc[3])
            for j in range(4):
                for rb in range(4):
                    for q in range(NQ):
                        P.coll(BRl[j][rb, q], BRgl[j][rb, q], GROUPS)
                if with_ctx:
                    P.coll(BRx[j], BRgx[j], GROUPS)
            P.emit()
            merge_phase(env, l, PTmg, BRgl, BRgx, w_br, w_out, hc, hq, pv_ln, with_ctx)
            gather_h()
            proj_phase(env, l, ffn_upc[l], UT, 2 * DFQ, 4, 3, all_src if with_ctx else all_src[1:])
            ffn_act(env, l, UT, pv_ffn, ACTc, with_ctx)
            for rb in range(NBF):
                for q in range(NQ):
                    P.coll(ACl[rb, q], ACgl[rb, q], GROUPS)
            if with_ctx:
                P.coll(ACx, ACgx, GROUPS)
            P.emit()
            ffn_down_phase(env, l, ACgl, ACgx, ffn_down, hc, hq, pv_ln, with_ctx)

        with contextlib.ExitStack() as st:
            t, bt = kb.sb(st, "cpo", [128, 2048])
            for k in range(NCH):
                for cc in range(NCC):
                    kb.ld("sp", t[:, :CWH], lambda i, k=k, cc=cc: hq[k, cc, :, :], [bt])
                    kb.stq("sp", lambda i, k=k, cc=cc: out[k, cc, :, :], t[:, :CWH], [bt])
            P.emit()
    return nc


def proj_phase(env, l, w, dst, ncol, sc_which, sh_which, sources):
    nc, P, kb, ada = env["nc"], env["P"], env["kb"], env["ada"]
    NB = (ncol + 127) // 128
    for sd in sources:
        seg, TT, T, src, dcol = sd["seg"], sd["TT"], sd["T"], sd["src"], sd["dcol"]
        with contextlib.ExitStack() as st:
            NH = (TT + 511) // 512
            HW = TT // NH
            xf, b_xf = kb.sb(st, "xf", [128, TT])
            xm = st.enter_context(nc.sbuf_tensor(U("xm"), [128, NCH, TT], BF16))
            b_xm = [P.buf() for _ in range(NCH)]
            sc1p, b_sc = kb.sb(st, "sc1p", [128, NCH])
            wt = [kb.sb(st, "wt", [128, NCH, 512], BF16) for k in range(3)]
            ev = [kb.sb(st, "ev", [128, TT]) for k in range(3)]
            ps = [kb.psb(st, "ps", [128, 512]) for k in range(6)]
            kb.ts("dve", sc1p[:, :], ada(l, seg, sc_which), 1.0, None, ALU.add, None, [], [b_sc])
            P.emit()
            for c in range(NCH):
                kb.ld("sp", xf[:, :], lambda i, c=c: src(c, i), [b_xf])
                kb.ts("dve", xm[:, c, :], xf[:, :], sc1p[:, c:c + 1], ada(l, seg, sh_which, c), ALU.mult, ALU.add,
                      [b_xf, b_sc], [b_xm[c]])
            pi = 0
            for g in range((NB + 3) // 4):
                k = g % 3
                c0 = g * 512
                cw = min(512, ncol - c0)
                kb.ld("pool", wt[k][0][:, :, :cw], lambda i, c0=c0, cw=cw: w[:, c0:c0 + cw]
                      .rearrange("(c p) n -> p c n", p=128), [wt[k][1]])
                for m in range((cw + 127) // 128):
                    mw = min(128, cw - m * 128)
                    e = (g * 4 + m) % 3
                    for hh in range(NH):
                        p_ = pi % 6
                        pi += 1
                        for c in range(NCH):
                            kb.mm(ps[p_][0][:mw, :HW], wt[k][0][:, c, m * 128:m * 128 + mw],
                                  xm[:, c, hh * HW:(hh + 1) * HW], c == 0, c == NCH - 1,
                                  [wt[k][1]] + b_xm, [ps[p_][1]])
                        kb.cp("act" if pi % 2 == 0 else "dve", ev[e][0][:mw, hh * HW:(hh + 1) * HW],
                              ps[p_][0][:mw, :HW], [ps[p_][1]], [ev[e][1]])
                    row0 = c0 + m * 128
                    kb.stq("sp", lambda i, mw=mw, row0=row0: dst[row0:row0 + mw, dcol(i):dcol(i) + TT],
                           ev[e][0][:mw, :], [ev[e][1]])
            P.emit(T)
def lru_phase(env, l, PT, pv_lru, lru_gw, BR):
    nc, P, kb, S, ST = env["nc"], env["P"], env["kb"], env["S"], env["ST"]
    TT = min(1024, env["TQ"])
    tiles = seg_tiles(env, TT)
    pvv = pv_lru[l].rearrange("b p k -> (b p) k")
    with contextlib.ExitStack() as st:
        pv, b_pv = kb.sb(st, "pv", [128, 12])
        gw = [kb.sb(st, "gw", [128, 128], BF16) for _ in range(4)]
        e1, b_e1 = kb.sb(st, "e1", [128, 2])
        m8, b_m8 = kb.sb(st, "m8", [128, 2])
        state, b_state = kb.sb(st, "state", [128, 1])
        yacc, b_yacc = kb.sb(st, "yacc", [128, ST])
        xh, b_xh = kb.sb(st, "xh", [128, TT + 3])
        u, b_u = kb.sb(st, "u", [128, TT])
        ub, b_ub = kb.sb(st, "ub", [128, TT], BF16)
        rr, b_rr = kb.sb(st, "rr", [128, TT])
        ii, b_ii = kb.sb(st, "ii", [128, TT])
        aa, b_aa = kb.sb(st, "aa", [128, TT])
        t1, b_t1 = kb.sb(st, "t1", [128, TT])
        t2, b_t2 = kb.sb(st, "t2", [128, TT])
        gt, b_gt = kb.sb(st, "gt", [128, TT])
        ob, b_ob = kb.sb(st, "ob", [128, TT], BF16)
        ps = [kb.psb(st, "psl", [128, 512]) for _ in range(4)]
        kb.ld("sp", pv[:, :], lambda i: pvv[i * 128:(i + 1) * 128, :], [b_pv])
        for k in range(4):
            kb.ld("pool", gw[k][0][:, :], lambda i, k=k: lru_gw[l, i * 128 + k * 512:i * 128 + k * 512 + 128, :], [gw[k][1]])
        kb.act(e1[:, :], pv[:, 9:11], AF.Exp, [b_pv], [b_e1], scale=-1.0)
        kb.act(e1[:, :], e1[:, :], AF.Ln, [b_e1], [b_e1], bias=1.0)
        kb.ts("dve", m8[:, :], e1[:, :], -8.0, None, ALU.mult, None, [b_e1], [b_m8])
        pi = 0
        for d in range(2):
            kb.ms("dve", state[:, :], 0.0, [b_state])
            for sg in tiles:
                for (seg, col, tok, tn) in (sg if d == 0 else sg[::-1]):
                    NH = (tn + 511) // 512
                    HW = tn // NH
                    kb.ld("sp", xh[:, :tn + 3], lambda i, col=col, tn=tn: PT[i * 128 + LO_AX:i * 128 + LO_AX + 128, col - 2:col + tn + 1], [b_xh])
                    kb.ts("dve", u[:, :tn], xh[:, 0:tn], pv[:, 0:1], pv[:, 4:5], ALU.mult, ALU.add, [b_xh, b_pv], [b_u])
                    for k in range(1, 4):
                        kb.stt("dve", u[:, :tn], xh[:, k:k + tn], pv[:, k:k + 1], u[:, :tn], ALU.mult, ALU.add,
                               [b_xh, b_pv, b_u], [b_u])
                    kb.cp("act", ub[:, :tn], u[:, :tn], [b_u], [b_ub])
                    for wh, (dst, b_dst) in enumerate(((rr, b_rr), (ii, b_ii))):
                        for hh in range(NH):
                            p_ = pi % 4
                            pi += 1
                            kb.mm(ps[p_][0][:, :HW], gw[d * 2 + wh][0][:, :], ub[:, hh * HW:(hh + 1) * HW], True, True,
                                  [gw[d * 2 + wh][1], b_ub], [ps[p_][1]])
                            kb.act(dst[:, hh * HW:(hh + 1) * HW], ps[p_][0][:, :HW], AF.Sigmoid, [ps[p_][1], b_pv], [b_dst],
                                   bias=pv[:, 5 + d * 2 + wh:6 + d * 2 + wh])
                    kb.act(aa[:, :tn], rr[:, :tn], AF.Exp, [b_rr, b_m8], [b_aa], scale=m8[:, d:d + 1])
                    kb.stt("dve", t1[:, :tn], aa[:, :tn], -1.0, aa[:, :tn], ALU.mult, ALU.mult, [b_aa], [b_t1])
                    kb.act(t1[:, :tn], t1[:, :tn], AF.Sqrt, [b_t1], [b_t1], bias=1.0)
                    kb.tt("pool", t2[:, :tn], ii[:, :tn], u[:, :tn], ALU.mult, [b_ii, b_u], [b_t2])
                    kb.tt("dve", t2[:, :tn], t2[:, :tn], t1[:, :tn], ALU.mult, [b_t1, b_t2], [b_t2])
                    if d == 0:
                        o_ap = yacc[:, tok:tok + tn]
                        P.op("dve", lambda i, o_ap=o_ap, tn=tn: nc.vector.tensor_tensor_scan(
                            out=o_ap, data0=aa[:, :tn], data1=t2[:, :tn], initial=state[:, 0:1],
                            op0=ALU.mult, op1=ALU.add), [b_aa, b_t2, b_state], [b_yacc])
                        kb.cp("dve", state[:, :], yacc[:, tok + tn - 1:tok + tn], [b_yacc], [b_state])
                    else:
                        P.op("dve", lambda i, tn=tn: nc.vector.tensor_tensor_scan(
                            out=rr[:, tn - 1::-1] if False else rr[:, 0:tn][:, ::-1], data0=aa[:, 0:tn][:, ::-1],
                            data1=t2[:, 0:tn][:, ::-1], initial=state[:, 0:1],
                            op0=ALU.mult, op1=ALU.add), [b_aa, b_t2, b_state], [b_rr])
                        kb.cp("dve", state[:, :], rr[:, 0:1], [b_rr], [b_state])
                        kb.tt("dve", rr[:, :tn], rr[:, :tn], yacc[:, tok:tok + tn], ALU.add, [b_rr, b_yacc], [b_rr])
                        kb.ld("sp", gt[:, :tn], lambda i, col=col, tn=tn: PT[i * 128 + LO_AG:i * 128 + LO_AG + 128, col:col + tn], [b_gt])
                        kb.tt("pool", t1[:, :tn], gt[:, :tn], gt[:, :tn], ALU.mult, [b_gt], [b_t1])
                        kb.ts("dve", t1[:, :tn], t1[:, :tn], 0.044715, 1.0, ALU.mult, ALU.add, [b_t1], [b_t1])
                        kb.tt("dve", t1[:, :tn], t1[:, :tn], gt[:, :tn], ALU.mult, [b_t1, b_gt], [b_t1])
                        kb.act(t1[:, :tn], t1[:, :tn], AF.Sigmoid, [b_t1], [b_t1], scale=1.5957691216057308)
                        kb.tt("pool", t1[:, :tn], t1[:, :tn], gt[:, :tn], ALU.mult, [b_t1, b_gt], [b_t1])
                        kb.tt("dve", ob[:, :tn], t1[:, :tn], rr[:, :tn], ALU.mult, [b_t1, b_rr], [b_ob])
                        kb.stq("sp", lambda i, tok=tok, tn=tn: BR[i * 128:(i + 1) * 128, tok:tok + tn], ob[:, :tn], [b_ob])
        P.emit(4)


def sconv_phase(env, l, PT, pv_sc, BR):
    nc, P, kb, S, ST = env["nc"], env["P"], env["kb"], env["S"], env["ST"]
    TT = min(1024, env["TQ"])
    tiles = seg_tiles(env, TT)
    pvv = pv_sc[l].rearrange("b p k -> (b p) k")
    with contextlib.ExitStack() as st:
        pv, b_pv = kb.sb(st, "pvs", [128, 4])
        sbt, b_sbt = kb.sb(st, "sbt", [128, TT])
        sgh, b_sgh = kb.sb(st, "sgh", [128, TT + 2])
        sxh, b_sxh = kb.sb(st, "sxh", [128, TT + 2])
        o, b_o = kb.sb(st, "o", [128, TT])
        ob, b_ob = kb.sb(st, "obs", [128, TT], BF16)
        kb.ld("sp", pv[:, :], lambda i: pvv[i * 128:(i + 1) * 128, :], [b_pv])
        for sg in tiles:
            for (seg, col, tok, tn) in sg:
                kb.ld("sp", sbt[:, :tn], lambda i, col=col, tn=tn: PT[i * 128 + LO_SB:i * 128 + LO_SB + 128, col:col + tn], [b_sbt])
                kb.ld("sp", sgh[:, :tn + 2], lambda i, col=col, tn=tn: PT[i * 128 + LO_SG:i * 128 + LO_SG + 128, col - 1:col + tn + 1], [b_sgh])
                kb.ld("sp", sxh[:, :tn + 2], lambda i, col=col, tn=tn: PT[i * 128 + LO_SX:i * 128 + LO_SX + 128, col - 1:col + tn + 1], [b_sxh])
                kb.tt("pool", sgh[:, :tn + 2], sgh[:, :tn + 2], sxh[:, :tn + 2], ALU.mult, [b_sgh, b_sxh], [b_sgh])
                kb.ts("dve", o[:, :tn], sgh[:, 0:tn], pv[:, 0:1], None, ALU.mult, None, [b_sgh, b_pv], [b_o])
                for k in (1, 2):
                    kb.stt("dve", o[:, :tn], sgh[:, k:k + tn], pv[:, k:k + 1], o[:, :tn], ALU.mult, ALU.add,
                           [b_sgh, b_pv, b_o], [b_o])
                kb.tt("dve", ob[:, :tn], o[:, :tn], sbt[:, :tn], ALU.mult, [b_o, b_sbt], [b_ob])
                kb.stq("sp", lambda i, tok=tok, tn=tn: BR[i * 128:(i + 1) * 128, tok:tok + tn], ob[:, :tn], [b_ob])
        P.emit(4)


def attn_prep(env, l, PT, pv_att, ropeT, QTg, QTc, KT, VTM, with_ctx):
    nc, P, kb, S, ST, cst = env["nc"], env["P"], env["kb"], env["S"], env["ST"], env["cst"]
    IDENT, RMAT, ONES = cst[:, 0, :], cst[:, 1, :], cst[:, 2, :]
    for (seg, col0, tok0, ntok) in env["SEGS"]:
        TT = min(512, ntok)
        T = ntok // TT
        NS = TT // 128
        lat = seg == 1
        with contextlib.ExitStack() as st:
            pv, b_pv = kb.sb(st, "pva", [128, 2])
            cs, b_cs = kb.sb(st, "cs", [128, TT])
            sn, b_sn = kb.sb(st, "sn", [128, TT])
            qf = [kb.sb(st, "qf", [128, TT]) for _ in range(2)]
            sq, b_sq = kb.sb(st, "sq", [128, TT])
            rs, b_rs = kb.sb(st, "rs", [128, TT])
            qn, b_qn = kb.sb(st, "qn", [128, TT])
            t1, b_t1 = kb.sb(st, "t1a", [128, TT])
            t2, b_t2 = kb.sb(st, "t2a", [128, TT])
            qo = [kb.sb(st, "qo", [128, TT], BF16) for _ in range(2)]
            vb, b_vb = kb.sb(st, "vb", [128, NS, 128], BF16)
            ps1, b_ps1 = kb.psb(st, "pa1", [128, 512])
            ps2, b_ps2 = kb.psb(st, "pa2", [128, 512])
            pst = [kb.psb(st, "pat", [128, 128]) for _ in range(2)]
            kb.ld("sp", pv[:, :], lambda i: pv_att[l], [b_pv])
            if lat:
                kb.ld("sp", cs[:, :], lambda i: ropeT[0, :, i * TT:(i + 1) * TT], [b_cs])
                kb.ld("sp", sn[:, :], lambda i: ropeT[1, :, i * TT:(i + 1) * TT], [b_sn])
            heads = [("q", h) for h in range(4)] if (lat or with_ctx) else []
            heads += [("k", 0)]
            for n_, (kind, h) in enumerate(heads):
                row = (LO_Q if kind == "q" else LO_K) + h * 128
                qf_, b_qf = qf[n_ % 2]
                qo_, b_qo = qo[n_ % 2]
                kb.ld("sp", qf_[:, :], lambda i, row=row: PT[row:row + 128, i * TT + col0:i * TT + col0 + TT], [b_qf])
                kb.tt("pool", sq[:, :], qf_[:, :], qf_[:, :], ALU.mult, [b_qf], [b_sq])
                kb.mm(ps1[:, :TT], ONES, sq[:, :], True, True, [b_sq], [b_ps1])
                kb.act(rs[:, :], ps1[:, :TT], AF.Sqrt, [b_ps1], [b_rs], bias=RMS_EPS, scale=1.0 / 128)
                P.op("dve", lambda i: nc.vector.reciprocal(out=rs[:, :], in_=rs[:, :]), [b_rs], [b_rs])
                gcol = pv[:, 0:1] if kind == "q" else pv[:, 1:2]
                if lat:
                    kb.stt("dve", qn[:, :], qf_[:, :], gcol, rs[:, :], ALU.mult, ALU.mult, [b_qf, b_pv, b_rs], [b_qn])
                    kb.mm(ps2[:, :TT], RMAT, qn[:, :], True, True, [b_qn], [b_ps2])
                    kb.tt("pool", t1[:, :], qn[:, :], cs[:, :], ALU.mult, [b_qn, b_cs], [b_t1])
                    kb.tt("dve", t2[:, :], ps2[:, :TT], sn[:, :], ALU.mult, [b_ps2, b_sn], [b_t2])
                    kb.tt("dve", qo_[:, :], t1[:, :], t2[:, :], ALU.add, [b_t1, b_t2], [b_qo])
                else:
                    kb.stt("dve", qo_[:, :], qf_[:, :], gcol, rs[:, :], ALU.mult, ALU.mult, [b_qf, b_pv, b_rs], [b_qo])
                if kind == "q" and lat:
                    kb.stq("sp", lambda i, h=h: QTg[:, i * TT + h * S:i * TT + h * S + TT], qo_[:, :], [b_qo])
                elif kind == "q":
                    kb.stq("sp", lambda i, h=h: QTc[h, :, :], qo_[:, :], [b_qo])
                else:
                    kb.stq("sp", lambda i, h=h: KT[:, i * TT + tok0:i * TT + tok0 + TT], qo_[:, :], [b_qo])
            for h in range(1):
                qf_, b_qf = qf[h % 2]
                kb.ld("sp", qf_[:, :], lambda i, h=h: PT[LO_V + h * 128:LO_V + (h + 1) * 128, i * TT + col0:i * TT + col0 + TT], [b_qf])
                for s_ in range(NS):
                    pt_, b_pt = pst[(h * NS + s_) % 2]
                    kb.tr(pt_[:, :], qf_[:, s_ * 128:(s_ + 1) * 128], IDENT, [b_qf], [b_pt])
                    kb.cp("act" if s_ % 2 else "dve", vb[:, s_, h * 128:(h + 1) * 128], pt_[:, :], [b_pt], [b_vb])
            kb.stq("sp", lambda i: VTM[i * TT + tok0:i * TT + tok0 + TT, :].rearrange("(s p) c -> p s c", p=128), vb[:, :, :], [b_vb])
            P.emit(T)


def attn_main(env, l, QTg, QTc, KT, VTM, BR2, with_ctx):
    nc, P, kb, S, ST, cstb = env["nc"], env["P"], env["kb"], env["S"], env["ST"], env["cstb"]
    ONESB = cstb[:, 2, :]
    NKC = ST // 128
    SCALE = 128.0 ** -0.5
    NQT = S // 512
    with contextlib.ExitStack() as st:
        Kt, b_Kt = kb.sb(st, "Kt", [128, ST], BF16)
        Vt, b_Vt = kb.sb(st, "Vt", [128, NKC, 128], BF16)
        Qt = [kb.sb(st, "Qt", [128, 512], BF16) for _ in range(2)]
        pt = [kb.sb(st, "pt", [128, 512], BF16) for _ in range(4)]
        rden, b_rden = kb.sb(st, "rden", [128, 512])
        acc = [kb.sb(st, "acc", [128, 512]) for _ in range(2)]
        ONESF = env["cst"][:, 2, :]
        ob, b_ob = kb.sb(st, "oba", [128, 512], BF16)
        pss = [kb.psb(st, "pss", [128, 512]) for _ in range(3)]
        pso, b_pso = kb.psb(st, "pso", [128, 512])
        psd, b_psd = kb.psb(st, "psd", [128, 512])
        kb.ld("sp", Kt[:, :], lambda i: KT, [b_Kt])
        kb.ld("sp", Vt[:, :, :], lambda i: VTM.rearrange("(c p) d -> p c d", p=128), [b_Vt])
        P.emit()

        def attend(qw, kcs, out_fn, q_fn):
            Qt_, b_Qt = Qt[0]
            kb.ld("sp", Qt_[:, :qw], q_fn, [b_Qt])
            n = len(kcs)

            def s_mm(n_):
                ps_, b_ps = pss[n_ % 3]
                kc = kcs[n_]
                kb.mm(ps_[:, :qw], Kt[:, kc * 128:(kc + 1) * 128], Qt_[:, :qw], True, True, [b_Kt, b_Qt], [b_ps])

            s_mm(0)
            for n_, kc in enumerate(kcs):
                ps_, b_ps = pss[n_ % 3]
                pt_, b_pt = pt[n_ % 4]
                if n_ + 1 < n:
                    s_mm(n_ + 1)
                kb.act(pt_[:, :qw], ps_[:, :qw], AF.Exp, [b_ps], [b_pt], scale=SCALE)
                kb.mm(pso[:, :qw], Vt[:, kc, :], pt_[:, :qw], n_ == 0, n_ == n - 1, [b_Vt, b_pt], [b_pso])
                e_ = n_ % 2
                acc_, b_acc = acc[e_]
                eng = "dve" if e_ == 0 else "pool"
                if n_ < 2:
                    kb.cp(eng, acc_[:, :qw], pt_[:, :qw], [b_pt], [b_acc])
                else:
                    kb.tt(eng, acc_[:, :qw], acc_[:, :qw], pt_[:, :qw], ALU.add, [b_pt, b_acc], [b_acc])
            if n > 1:
                kb.tt("dve", acc[0][0][:, :qw], acc[0][0][:, :qw], acc[1][0][:, :qw], ALU.add, [acc[0][1], acc[1][1]], [acc[0][1]])
            kb.mm(psd[:, :qw], ONESF, acc[0][0][:, :qw], True, True, [acc[0][1]], [b_psd])
            P.op("dve", lambda i: nc.vector.reciprocal(out=rden[:, :qw], in_=psd[:, :qw]), [b_psd], [b_rden])
            kb.tt("dve", ob[:, :qw], pso[:, :qw], rden[:, :qw], ALU.mult, [b_pso, b_rden], [b_ob])
            kb.stq("sp", out_fn, ob[:, :qw], [b_ob])

        TQ = env["TQ"]
        QW = min(512, TQ)
        for qh in range(4):
            for qq in range(NQ):
                attend(QW, list(range(NKC)),
                       lambda i, qh=qh, qq=qq: BR2[qh * 128:(qh + 1) * 128, CTX + qq * TQ + i * QW:CTX + qq * TQ + (i + 1) * QW],
                       lambda i, qh=qh, qq=qq: QTg[:, qh * S + qq * TQ + i * QW:qh * S + qq * TQ + (i + 1) * QW])
                P.emit(TQ // QW)
        if with_ctx:
            for qh in range(4):
                attend(CTX, list(range(CTX // 128)), lambda i, qh=qh: BR2[qh * 128:(qh + 1) * 128, 0:CTX],
                       lambda i, qh=qh: QTc[qh, :, :])
            P.emit()


def ssd_conv(env, l, PT, pv_ssc, U3):
    nc, P, kb, S = env["nc"], env["P"], env["kb"], env["S"]
    TT = min(1024, env["TQ"])
    tiles = seg_tiles(env, TT)
    pvv = pv_ssc[l].rearrange("b p k -> (b p) k")
    with contextlib.ExitStack() as st:
        pv, b_pv = kb.sb(st, "pvc", [128, 8])
        xh, b_xh = kb.sb(st, "xhc", [128, TT + 3])
        u, b_u = kb.sb(st, "uc", [128, TT])
        kb.ld("sp", pv[:, :], lambda i: pvv[i * 128:(i + 1) * 128, :], [b_pv])
        for sg in tiles:
            for (seg, col, tok, tn) in sg:
                kb.ld("sp", xh[:, :tn + 3], lambda i, col=col, tn=tn: PT[i * 128 + LO_XBC:i * 128 + LO_XBC + 128, col - 2:col + tn + 1], [b_xh])
                kb.ts("dve", u[:, :tn], xh[:, 0:tn], pv[:, 0:1], pv[:, 4:5], ALU.mult, ALU.add, [b_xh, b_pv], [b_u])
                for k in range(1, 4):
                    kb.stt("dve", u[:, :tn], xh[:, k:k + tn], pv[:, k:k + 1], u[:, :tn], ALU.mult, ALU.add,
                           [b_xh, b_pv, b_u], [b_u])
                kb.act(u[:, :tn], u[:, :tn], AF.Silu, [b_u], [b_u])
                kb.stq("sp", lambda i, tok=tok, tn=tn: U3[i * 128:(i + 1) * 128, tok:tok + tn], u[:, :tn], [b_u])
        P.emit(6)


def ssd_main(env, l, PT, U3, pv_dt, pv_ssd, YS):
    nc, P, kb, S, cst = env["nc"], env["P"], env["kb"], env["S"], env["cst"]
    IDENT, ONES = cst[:, 0, :], cst[:, 2, :]
    NHD, NXB = 8, 4
    with contextlib.ExitStack() as st0:
        pvd, b_pvd = kb.sb(st0, "pvd", [128, 32])
        aneg, b_aneg = kb.sb(st0, "aneg", [128, 16])
        dsk, b_dsk = kb.sb(st0, "dsk", [128, NXB, 2])
        H, b_H = kb.sb(st0, "H", [128, 512])
        kb.ld("sp", pvd[:, :], lambda i: pv_dt[l], [b_pvd])
        kb.ld("sp", dsk[:, :, :], lambda i: pv_ssd[l], [b_dsk])
        kb.act(aneg[:, :], pvd[:, 16:32], AF.Exp, [b_pvd], [b_aneg])
        kb.ts("dve", aneg[:, :], aneg[:, :], -1.0, None, ALU.mult, None, [b_aneg], [b_aneg])
        P.emit()
        for d in range(2):
            TRI, MASK, NEG = (cst[:, 3, :], cst[:, 4, :], cst[:, 5, :]) if d == 0 else (cst[:, 6, :], cst[:, 7, :], cst[:, 8, :])
            kb.ms("dve", H[:, :], 0.0, [b_H])
            P.emit()
            for (seg, col0, tok0, ntok) in env["SEGS"]:
                T = ntok // 128
                with contextlib.ExitStack() as st:
                    xT_, b_xT = kb.sb(st, "xT_", [128, NXB, 128])
                    BT_, b_BT = kb.sb(st, "BT_", [128, 128])
                    CT_, b_CT = kb.sb(st, "CT_", [128, 128])
                    dtT, b_dtT = kb.sb(st, "dtT", [16, 128])
                    yprev, b_yprev = kb.sb(st, "yprev", [128, NXB, 128])
                    dtv, b_dtv = kb.sb(st, "dtv", [128, NHD])
                    av, b_av = kb.sb(st, "av", [128, NHD])
                    acs, b_acs = kb.sb(st, "acs", [128, NHD])
                    dec, b_dec = kb.sb(st, "dec", [128, NHD])
                    cdb, b_cdb = kb.sb(st, "cdb", [128, NHD])
                    xx, b_xx = kb.sb(st, "xx", [128, 512], BF16)
                    xxd, b_xxd = kb.sb(st, "xxd", [128, 512], BF16)
                    Btm, b_Btm = kb.sb(st, "Btm", [128, 128], BF16)
                    Bb, b_Bb = kb.sb(st, "Bb", [128, 128], BF16)
                    Cb, b_Cb = kb.sb(st, "Cb", [128, 128], BF16)
                    Gs, b_Gs = kb.sb(st, "Gs", [128, 128])
                    A1 = [kb.sb(st, "A1", [128, 128]) for _ in range(2)]
                    Ab = [kb.sb(st, "Ab", [128, 128]) for _ in range(2)]
                    Ee = [kb.sb(st, "Ee", [128, 128]) for _ in range(2)]
                    EA = [kb.sb(st, "EA", [128, 128]) for _ in range(2)]
                    Mb = [kb.sb(st, "Mb", [128, 128], BF16) for _ in range(2)]
                    Cs = [kb.sb(st, "Cs", [128, 128]) for _ in range(2)]
                    ysb, b_ysb = kb.sb(st, "ysb", [128, NXB, 128])
                    psx, b_psx = kb.psb(st, "psx", [128, 512])
                    psm, b_psm = kb.psb(st, "psm", [128, 128])
                    psb_, b_psb = kb.psb(st, "psbb", [128, 256])
                    psDE_t = st.enter_context(nc.psum_tensor(U("psDE"), [128, 512], F32))
                    psDE = [(psDE_t[:, 0:256], P.buf()), (psDE_t[:, 256:512], P.buf())]
                    psY = st.enter_context(nc.psum_tensor(U("psY"), [128, 512], F32))
                    b_psY = [P.buf() for _ in range(4)]

                    def tokf(i):
                        return tok0 + (i if d == 0 else T - 1 - i) * 128

                    def colf(i):
                        return col0 + (i if d == 0 else T - 1 - i) * 128

                    kb.ld("sp", xT_[:, :, :], lambda i: U3[0:512, tokf(i):tokf(i) + 128].rearrange("(b p) t -> p b t", p=128), [b_xT])
                    kb.ld("sp", BT_[:, :], lambda i: U3[512:640, tokf(i):tokf(i) + 128], [b_BT])
                    kb.ld("sp", CT_[:, :], lambda i: U3[640:768, tokf(i):tokf(i) + 128], [b_CT])
                    kb.ld("sp", dtT[:, :], lambda i: PT[LO_DT:LO_DT + 16, colf(i):colf(i) + 128], [b_dtT])
                    if d == 1:
                        kb.ld("sp", yprev[:, :, :], lambda i: YS[:, tokf(i):tokf(i) + 128].rearrange("(b p) t -> p b t", p=128), [b_yprev])
                    kb.tr(psm[:, 0:16], dtT[:, :], IDENT[0:16, 0:16], [b_dtT], [b_psm])
                    kb.tt("dve", dtv[:, :], psm[:, d * 8:(d + 1) * 8], pvd[:, d * 8:(d + 1) * 8], ALU.add, [b_psm, b_pvd], [b_dtv])
                    kb.act(dtv[:, :], dtv[:, :], AF.Exp, [b_dtv], [b_dtv])
                    kb.act(dtv[:, :], dtv[:, :], AF.Ln, [b_dtv], [b_dtv], bias=1.0)
                    kb.tt("dve", av[:, :], dtv[:, :], aneg[:, d * 8:(d + 1) * 8], ALU.mult, [b_dtv, b_aneg], [b_av])
                    kb.mm(psm[:, 64:72], TRI, av[:, :], True, True, [b_av], [b_psm])
                    kb.mm(psm[:, 96:104], ONES, av[:, :], True, True, [b_av], [b_psm])
                    kb.cp("act", acs[:, :], psm[:, 64:72], [b_psm], [b_acs])
                    kb.tt("dve", dec[:, :], psm[:, 96:104], acs[:, :], ALU.subtract, [b_psm, b_acs], [b_dec])
                    kb.act(dec[:, :], dec[:, :], AF.Exp, [b_dec], [b_dec])
                    kb.act(cdb[:, :], psm[:, 96:104], AF.Exp, [b_psm], [b_cdb])
                    for b in range(NXB):
                        kb.tr(psx[:, b * 128:(b + 1) * 128], xT_[:, b, :], IDENT, [b_xT], [b_psx])
                    kb.tt("dve", xx[:, :].rearrange("p (h e) -> p h e", e=64),
                          psx[:, :].rearrange("p (h e) -> p h e", e=64), bc_last(dtv[:, :], 64),
                          ALU.mult, [b_psx, b_dtv], [b_xx])
                    kb.tt("pool", xxd[:, :].rearrange("p (h e) -> p h e", e=64),
                          xx[:, :].rearrange("p (h e) -> p h e", e=64), bc_last(dec[:, :], 64),
                          ALU.mult, [b_xx, b_dec], [b_xxd])
                    kb.tr(psb_[:, 0:128], BT_[:, :], IDENT, [b_BT], [b_psb])
                    kb.cp("act", Btm[:, :], psb_[:, 0:128], [b_psb], [b_Btm])
                    kb.cp("pool", Bb[:, :], BT_[:, :], [b_BT], [b_Bb])
                    kb.cp("pool", Cb[:, :], CT_[:, :], [b_CT], [b_Cb])
                    kb.mm(psb_[:, 128:256], Bb[:, :], Cb[:, :], True, True, [b_Bb, b_Cb], [b_psb])
                    kb.cp("act", Gs[:, :], psb_[:, 128:256], [b_psb], [b_Gs])
                    for h in range(NHD):
                        k = h % 2
                        kb.ts("dve", A1[k][0][:, :], MASK, av[:, h:h + 1], None, ALU.mult, None, [b_av], [A1[k][1]])
                        kb.act(Ab[k][0][:, :], ONES, AF.Copy, [b_av], [Ab[k][1]], scale=av[:, h:h + 1])
                        pD, b_pD = psDE[k]
                        kb.mm(pD[:, 0:128], A1[k][0][:, :], TRI, True, False, [A1[k][1]], [b_pD])
                        kb.mm(pD[:, 0:128], IDENT, NEG, False, True, [], [b_pD])
                        kb.mm(pD[:, 128:256], Ab[k][0][:, :], TRI, True, True, [Ab[k][1]], [b_pD])
                        kb.act(Ee[k][0][:, :], pD[:, 0:128], AF.Exp, [b_pD], [Ee[k][1]])
                        kb.act(EA[k][0][:, :], pD[:, 128:256], AF.Exp, [b_pD], [EA[k][1]])
                        kb.tt("dve", Mb[k][0][:, :], Ee[k][0][:, :], Gs[:, :], ALU.mult, [Ee[k][1], b_Gs], [Mb[k][1]])
                        kb.tt("pool", Cs[k][0][:, :], CT_[:, :], EA[k][0][:, :], ALU.mult, [EA[k][1], b_CT], [Cs[k][1]])
                        slot = (h // 2) % 4
                        yo = psY[(h % 2) * 64:(h % 2) * 64 + 64, slot * 128:(slot + 1) * 128]
                        kb.mm(yo, xx[:, h * 64:(h + 1) * 64], Mb[k][0][:, :], True, False, [b_xx, Mb[k][1]], [b_psY[slot]])
                        kb.mm(yo, H[:, h * 64:(h + 1) * 64], Cs[k][0][:, :], False, True, [b_H, Cs[k][1]], [b_psY[slot]])
                        if h % 2 == 1:
                            b = h // 2
                            ysl = psY[:, slot * 128:(slot + 1) * 128]
                            if d == 0:
                                kb.stt("dve", ysb[:, b, :], xT_[:, b, :], dsk[:, b, 1:2], ysl, ALU.mult, ALU.add,
                                       [b_xT, b_dsk, b_psY[slot]], [b_ysb])
                            else:
                                kb.tt("dve", ysb[:, b, :], ysl, yprev[:, b, :], ALU.add, [b_psY[slot], b_yprev], [b_ysb])
                    kb.stq("sp", lambda i: YS[:, tokf(i):tokf(i) + 128].rearrange("(b p) t -> p b t", p=128), ysb[:, :, :], [b_ysb])
                    kb.mm(psx[:, :], Btm[:, :], xxd[:, :], True, True, [b_Btm, b_xxd], [b_psx])
                    kb.tt("pool", H[:, :].rearrange("p (h e) -> p h e", e=64), H[:, :].rearrange("p (h e) -> p h e", e=64),
                          bc_last(cdb[:, :], 64), ALU.mult, [b_H, b_cdb], [b_H])
                    kb.tt("dve", H[:, :], H[:, :], psx[:, :], ALU.add, [b_H, b_psx], [b_H])
                    P.emit(T)


def ssd_norm(env, l, PT, YS, pv_ssd, BR3):
    nc, P, kb, S, cst = env["nc"], env["P"], env["kb"], env["S"], env["cst"]
    ONES = cst[:, 2, :]
    TQ = env["TQ"]
    segs = [env["SEGS"][0]] + [(1, env["L0"] + q * TQ, CTX + q * TQ, TQ) for q in range(NQ)]
    for (seg, col0, tok0, ntok) in segs:
        TT = min(512, ntok)
        T = ntok // TT
        with contextlib.ExitStack() as st:
            dsk, b_dsk = kb.sb(st, "dskn", [128, 4, 2])
            yg, b_yg = kb.sb(st, "yg", [128, 4, TT])
            y, b_y = kb.sb(st, "yn", [128, TT])
            z, b_z = kb.sb(st, "zn", [128, TT])
            sq, b_sq = kb.sb(st, "sqn", [128, TT])
            rs, b_rs = kb.sb(st, "rsn", [128, TT])
            ob, b_ob = kb.sb(st, "obn", [128, TT], BF16)
            ps, b_ps = kb.psb(st, "psn", [128, 512])
            kb.ld("sp", dsk[:, :, :], lambda i: pv_ssd[l], [b_dsk])
            for b in range(4):
                kb.ld("sp", y[:, :], lambda i, b=b: YS[b * 128:(b + 1) * 128, i * TT + tok0:i * TT + tok0 + TT], [b_y])
                kb.ld("sp", z[:, :], lambda i, b=b: PT[LO_Z + b * 128:LO_Z + (b + 1) * 128, i * TT + col0:i * TT + col0 + TT], [b_z])
                kb.act(z[:, :], z[:, :], AF.Silu, [b_z], [b_z])
                kb.tt("dve", yg[:, b, :], y[:, :], z[:, :], ALU.mult, [b_y, b_z], [b_yg])
                kb.tt("pool", sq[:, :], yg[:, b, :], yg[:, b, :], ALU.mult, [b_yg], [b_sq])
                kb.mm(ps[:, :TT], ONES, sq[:, :], b == 0, b == 3, [b_sq], [b_ps])
            kb.act(rs[:, :], ps[:, :TT], AF.Sqrt, [b_ps], [b_rs], bias=RMS_EPS, scale=1.0 / 512)
            P.op("dve", lambda i: nc.vector.reciprocal(out=rs[:, :], in_=rs[:, :]), [b_rs], [b_rs])
            for b in range(4):
                kb.stt("dve", ob[:, :], yg[:, b, :], dsk[:, b, 0:1], rs[:, :], ALU.mult, ALU.mult, [b_yg, b_dsk, b_rs], [b_ob])
                kb.stq("sp", lambda i, b=b: BR3[b * 128:(b + 1) * 128, i * TT + tok0:i * TT + tok0 + TT], ob[:, :], [b_ob])
            P.emit(T)


ALU_ALPHA = ALPHA


class HRes:
    def __init__(self, hc, hq, seg, cc):
        self.hc, self.hq, self.seg, self.cc = hc, hq, seg, cc

    def __getitem__(self, key):
        rs, cs = key
        if self.seg == 0:
            return self.hc[rs, cs]
        return self.hq[rs.start // 128, self.cc, :, cs]


def own_segs(env, with_ctx):
    CWH, NCC = env["CWH"], env["NCC"]
    res = []
    if with_ctx:
        res.append((0, 0, 0, CTX, 0))
    for cc in range(NCC):
        res.append((1, cc, CTX + cc * CWH, CWH, 1))
    return res


def merge_phase(env, l, PTmg, BRgl, BRgx, w_br, w_out, hc, hq, pv_ln, with_ctx):
    nc, P, kb, S, ada = env["nc"], env["P"], env["kb"], env["S"], env["ada"]
    TQ, CWH = env["TQ"], env["CWH"]
    for (seg, cc, mcol0, ntok, cj) in own_segs(env, with_ctx):
        TT = min(512, ntok)
        T = ntok // TT
        hres = HRes(hc, hq, seg, cc)
        with contextlib.ExitStack() as st:
            pln, b_pln = kb.sb(st, "pln", [128, 64])
            bt, b_bt = kb.sb(st, "bt", [128, NCH, TT], BF16)
            wt = [kb.sb(st, "wtm", [128, NCH, 512], BF16) for _ in range(2)]
            m, b_m = kb.sb(st, "mm_", [128, NCH, TT])
            mb, b_mb = kb.sb(st, "mb", [128, NCH, TT], BF16)
            r, b_r = kb.sb(st, "r", [128, NCH, TT])
            gt = [kb.sb(st, "gtm", [128, TT]) for _ in range(2)]
            tmp, b_tmp = kb.sb(st, "tmpm", [128, TT])
            ps = [kb.psb(st, "psm_", [128, 512]) for _ in range(4)]
            kb.ld("sp", pln[:, :], lambda i: pv_ln[l], [b_pln])

            def tokfn(i):
                return i * TT

            wi = 0
            n_ = 0
            for j in range(4):
                if seg == 0:
                    kb.ld("sp", bt[:, :, :], lambda i, j=j: BRgx[j].rearrange("r (b p) t -> p (r b) t", p=128), [b_bt])
                else:
                    for rb in range(4):
                        kb.ld("sp", bt[:, rb:NCH:4, :], lambda i, j=j, rb=rb: BRgl[j][rb, 0, :, :, cc * CWH + i * TT:cc * CWH + i * TT + TT]
                              .rearrange("r p t -> p r t"), [b_bt], cj=NQ * 128 * TQ)
                for og in range(4):
                    wt_, b_wt = wt[wi % 2]
                    wi += 1
                    kb.ld("pool", wt_[:, :, :], lambda i, j=j, og=og: w_br[l, j, :, og * 512:(og + 1) * 512].rearrange("(c p) n -> p c n", p=128), [b_wt])
                    for m_ in range(4):
                        ob = og * 4 + m_
                        ps_, b_ps = ps[n_ % 4]
                        gt_, b_gt = gt[n_ % 2]
                        n_ += 1
                        for c in range(NCH):
                            kb.mm(ps_[:, :TT], wt_[:, c, m_ * 128:(m_ + 1) * 128], bt[:, c, :], c == 0, c == NCH - 1, [b_wt, b_bt], [b_ps])
                        row = j * D + ob * 128
                        kb.ld("sp", gt_[:, :], lambda i, row=row: PTmg[row:row + 128, mcol0 + i * TT:mcol0 + i * TT + TT], [b_gt])
                        kb.act(gt_[:, :], gt_[:, :], AF.Sigmoid, [b_gt], [b_gt])
                        if j == 0:
                            kb.tt("dve", m[:, ob, :], ps_[:, :TT], gt_[:, :], ALU.mult, [b_ps, b_gt], [b_m])
                        else:
                            kb.tt("dve", tmp[:, :], ps_[:, :TT], gt_[:, :], ALU.mult, [b_ps, b_gt], [b_tmp])
                            kb.tt("pool", m[:, ob, :], m[:, ob, :], tmp[:, :], ALU.add, [b_tmp, b_m], [b_m])
            kb.cp("act", mb[:, :, :], m[:, :, :], [b_m], [b_mb])
            for og in range(4):
                wt_, b_wt = wt[wi % 2]
                wi += 1
                kb.ld("pool", wt_[:, :, :], lambda i, og=og: w_out[l, :, og * 512:(og + 1) * 512].rearrange("(c p) n -> p c n", p=128), [b_wt])
                for m_ in range(4):
                    ob = og * 4 + m_
                    ps_, b_ps = ps[n_ % 4]
                    gt_, b_gt = gt[n_ % 2]
                    n_ += 1
                    for c in range(NCH):
                        kb.mm(ps_[:, :TT], wt_[:, c, m_ * 128:(m_ + 1) * 128], mb[:, c, :], c == 0, c == NCH - 1, [b_wt, b_mb], [b_ps])
                    kb.ld("sp", gt_[:, :], lambda i, ob=ob: hres[ob * 128:(ob + 1) * 128, tokfn(i):tokfn(i) + TT], [b_gt])
                    kb.ts("dve", tmp[:, :], ps_[:, :TT], ada(l, seg, 2, ob), None, ALU.mult, None, [b_ps], [b_tmp])
                    kb.stt("dve", r[:, ob, :], gt_[:, :], ALU_ALPHA, tmp[:, :], ALU.mult, ALU.add, [b_gt, b_tmp], [b_r])
            ln_tail(env, st, l, 0, r, b_r, TT, hres, tokfn, pln, b_pln)
            P.emit(T)


def ffn_act(env, l, UT, pv_ffn, ACTc, with_ctx):
    nc, P, kb, S = env["nc"], env["P"], env["kb"], env["S"]
    TT = min(1024, env["TQ"])
    tiles = seg_tiles(env, TT)
    if not with_ctx:
        tiles = tiles[1:]
    pvv = pv_ffn[l].rearrange("b p k -> (b p) k")
    with contextlib.ExitStack() as st:
        pvg, b_pvg = kb.sb(st, "pvg", [128, 4])
        pvv_, b_pvv = kb.sb(st, "pvv", [128, 4])
        gh, b_gh = kb.sb(st, "gh", [128, TT + 2])
        vh, b_vh = kb.sb(st, "vh", [128, TT + 2])
        gc, b_gc = kb.sb(st, "gc", [128, TT])
        vc, b_vc = kb.sb(st, "vc", [128, TT])
        ob, b_ob = kb.sb(st, "obf", [128, TT], BF16)
        kb.ld("sp", pvg[:, :], lambda i: pvv[i * 128:(i + 1) * 128, :], [b_pvg])
        kb.ld("sp", pvv_[:, :], lambda i: pvv[NBF * 128 + i * 128:NBF * 128 + (i + 1) * 128, :], [b_pvv])
        for sg in tiles:
            for (seg, col, tok, tn) in sg:
                kb.ld("sp", gh[:, :tn + 2], lambda i, col=col, tn=tn: UT[i * 128:(i + 1) * 128, col - 1:col + tn + 1], [b_gh])
                kb.ld("sp", vh[:, :tn + 2], lambda i, col=col, tn=tn: UT[DFQ + i * 128:DFQ + (i + 1) * 128, col - 1:col + tn + 1], [b_vh])
                for (src, b_src, dst, b_dst, pv, b_pv) in ((gh, b_gh, gc, b_gc, pvg, b_pvg), (vh, b_vh, vc, b_vc, pvv_, b_pvv)):
                    kb.ts("dve", dst[:, :tn], src[:, 0:tn], pv[:, 0:1], pv[:, 3:4], ALU.mult, ALU.add, [b_src, b_pv], [b_dst])
                    for k in (1, 2):
                        kb.stt("dve", dst[:, :tn], src[:, k:k + tn], pv[:, k:k + 1], dst[:, :tn], ALU.mult, ALU.add,
                               [b_src, b_pv, b_dst], [b_dst])
                kb.act(gc[:, :tn], gc[:, :tn], AF.Silu, [b_gc], [b_gc])
                kb.tt("pool", ob[:, :tn], gc[:, :tn], vc[:, :tn], ALU.mult, [b_gc, b_vc], [b_ob])
                kb.stq("sp", lambda i, tok=tok, tn=tn: ACTc[i * 128:(i + 1) * 128, tok:tok + tn], ob[:, :tn], [b_ob])
        P.emit(NBF)


def ffn_down_phase(env, l, ACgl, ACgx, ffn_down, hc, hq, pv_ln, with_ctx):
    nc, P, kb, S, ada = env["nc"], env["P"], env["kb"], env["S"], env["ada"]
    NB = DFF // 128
    TQ, CWH = env["TQ"], env["CWH"]
    for (seg, cc, mcol0, ntok, cj) in own_segs(env, with_ctx):
        TT = min(512, ntok)
        T = ntok // TT
        hres = HRes(hc, hq, seg, cc)
        with contextlib.ExitStack() as st:
            pln, b_pln = kb.sb(st, "plnf", [128, 64])
            actb, b_actb = kb.sb(st, "actb", [128, NB, TT], BF16)
            wt = [kb.sb(st, "wtf", [128, NB, 256], BF16) for _ in range(2)]
            r, b_r = kb.sb(st, "rf", [128, NCH, TT])
            ht = [kb.sb(st, "htf", [128, TT]) for _ in range(2)]
            tmp, b_tmp = kb.sb(st, "tmpf", [128, TT])
            ps = [kb.psb(st, "psf", [128, 512]) for _ in range(4)]
            kb.ld("sp", pln[:, :], lambda i: pv_ln[l], [b_pln])

            def tokfn(i):
                return i * TT

            if seg == 0:
                kb.ld("sp", actb[:, :, :], lambda i: ACgx.rearrange("r (b p) t -> p (r b) t", p=128), [b_actb])
            else:
                for rb in range(NBF):
                    kb.ld("sp", actb[:, rb:NB:NBF, :], lambda i, rb=rb: ACgl[rb, 0, :, :, cc * CWH + i * TT:cc * CWH + i * TT + TT]
                          .rearrange("r p t -> p r t"), [b_actb], cj=NQ * 128 * TQ)
            n_ = 0
            for og in range(8):
                wt_, b_wt = wt[og % 2]
                kb.ld("pool", wt_[:, :, :], lambda i, og=og: ffn_down[l, :, og * 256:(og + 1) * 256].rearrange("(c p) n -> p c n", p=128), [b_wt])
                for m_ in range(2):
                    ob = og * 2 + m_
                    ps_, b_ps = ps[n_ % 4]
                    ht_, b_ht = ht[n_ % 2]
                    n_ += 1
                    for c in range(NB):
                        kb.mm(ps_[:, :TT], wt_[:, c, m_ * 128:(m_ + 1) * 128], actb[:, c, :], c == 0, c == NB - 1, [b_wt, b_actb], [b_ps])
                    kb.ld("sp", ht_[:, :], lambda i, ob=ob: hres[ob * 128:(ob + 1) * 128, tokfn(i):tokfn(i) + TT], [b_ht])
                    kb.ts("dve", tmp[:, :], ps_[:, :TT], ada(l, seg, 5, ob), None, ALU.mult, None, [b_ps], [b_tmp])
                    kb.stt("dve", r[:, ob, :], ht_[:, :], ALU_ALPHA, tmp[:, :], ALU.mult, ALU.add, [b_ht, b_tmp], [b_r])
            ln_tail(env, st, l, 1, r, b_r, TT, hres, tokfn, pln, b_pln)
            P.emit(T)


def make_consts(S):
    idx = np.arange(128)
    t_, l_ = idx[:, None], idx[None, :]
    c = np.zeros((128, 10, 128), np.float32)
    c[:, 0, :] = np.eye(128)
    partner = np.where((idx % 64) < 32, idx + 32, idx - 32)
    R = np.zeros((128, 128), np.float32)
    R[idx, partner] = 1.0
    c[:, 1, :] = R
    c[:, 2, :] = 1.0
    c[:, 3, :] = (t_ <= l_)
    c[:, 4, :] = (t_ > l_)
    c[:, 5, :] = np.where(l_ < t_, -30000.0, 0.0)
    c[:, 6, :] = (t_ >= l_)
    c[:, 7, :] = (t_ < l_)
    c[:, 8, :] = np.where(l_ > t_, -30000.0, 0.0)
    tok = np.arange(S)
    r = (tok // 64).astype(np.float32)
    cc = (tok % 64).astype(np.float32)
    inv = (np.float32(10000.0) ** (-np.arange(32, dtype=np.float32) / np.float32(32))).astype(np.float32)
    ang = np.zeros((128, S), np.float32)
    for d in range(128):
        pos = r if d < 64 else cc
        ang[d] = pos * inv[d % 32]
    sgn = np.where((idx % 64) < 32, -1.0, 1.0).astype(np.float32)
    rope = np.stack([np.cos(ang), np.sin(ang) * sgn[:, None]]).astype(np.float32)
    return c, rope


def pack_shared(inp, depth):
    L = depth
    f = lambda a: np.ascontiguousarray(a, dtype=np.float32)
    pv_ln = np.zeros((L, 128, 64), np.float32)
    pv_ln[:, :, 0:32] = inp["ln_g"][:L].reshape(L, 2, 16, 128).transpose(0, 3, 1, 2).reshape(L, 128, 32)
    pv_ln[:, :, 32:64] = inp["ln_b"][:L].reshape(L, 2, 16, 128).transpose(0, 3, 1, 2).reshape(L, 128, 32)
    return {
        "w_ada": f(inp["w_ada"][:L]),
        "w_mg": f(inp["w_in"][:L][..., 18496:26688]),
        "pv_att": f(np.stack([inp["attn_q_norm"][:L], inp["attn_k_norm"][:L]], -1)),
        "pv_ln": pv_ln,
        "w_br": f(np.stack([inp["w_br_lru"][:L], inp["w_br_sconv"][:L], inp["w_br_attn"][:L],
                            inp["w_br_ssm"][:L]], 1)),
        "w_out": f(inp["w_out"][:L]), "ffn_down": f(inp["ffn_down"][:L]),
    }


def pack_quarter(inp, depth, j):
    L = depth
    f = lambda a: np.ascontiguousarray(a, dtype=np.float32)
    w = inp["w_in"][:L]
    q5 = slice(j * 512, (j + 1) * 512)
    dtc = np.concatenate([np.arange(18432 + d * 32 + 8 * j, 18432 + d * 32 + 8 * j + 8) for d in range(2)])
    cols = np.concatenate([np.arange(o + j * 512, o + (j + 1) * 512) for o in (0, 2048, 4096, 6144, 8192, 10240)]
                          + [np.arange(12288 + j * 128, 12288 + (j + 1) * 128), np.arange(12800 + j * 128, 12800 + (j + 1) * 128),
                             np.arange(13312 + j * 512, 13312 + (j + 1) * 512), np.arange(15360 + j * 512, 15360 + (j + 1) * 512),
                             np.arange(17408 + j * 128, 17408 + (j + 1) * 128), np.arange(17920 + j * 128, 17920 + (j + 1) * 128), dtc])
    assert cols.shape[0] == NCOLC
    bl = slice(4 * j, 4 * j + 4)
    pv_lru = np.zeros((L, 4, 128, 12), np.float32)
    pv_lru[..., 0:4] = inp["lru_conv_w"][:L].reshape(L, 4, 16, 128).transpose(0, 2, 3, 1)[:, bl]
    pv_lru[..., 4] = inp["lru_conv_b"][:L].reshape(L, 16, 128)[:, bl]
    pv_lru[..., 5:9] = inp["lru_gate_b"][:L].reshape(L, 4, 16, 128).transpose(0, 2, 3, 1)[:, bl]
    pv_lru[..., 9:11] = inp["lru_lambda"][:L].reshape(L, 2, 16, 128).transpose(0, 2, 3, 1)[:, bl]
    gw = inp["lru_gate_w"][:L].reshape(L, 4, 16, 128, 128)[:, :, bl].reshape(L, 4 * 4 * 128, 128)
    pv_sc = np.zeros((L, 4, 128, 4), np.float32)
    pv_sc[..., 0:3] = inp["sconv_w"][:L].reshape(L, 3, 16, 128).transpose(0, 2, 3, 1)[:, bl]
    sblk = [4 * j, 4 * j + 1, 4 * j + 2, 4 * j + 3, 16 + j, 20 + j]
    pv_ssc = np.zeros((L, 6, 128, 8), np.float32)
    pv_ssc[..., 0:4] = inp["ssm_conv_w"][:L].reshape(L, 4, 24, 128).transpose(0, 2, 3, 1)[:, sblk]
    pv_ssc[..., 4] = inp["ssm_conv_b"][:L].reshape(L, 24, 128)[:, sblk]
    pv_ssd = np.zeros((L, 128, 4, 2), np.float32)
    pv_ssd[..., 0] = inp["ssm_norm"][:L].reshape(L, 16, 128)[:, bl].transpose(0, 2, 1)
    pv_ssd[..., 1] = np.repeat(inp["ssm_d"][:L], 64, axis=1).reshape(L, 16, 128)[:, bl].transpose(0, 2, 1)
    pv_dt = np.zeros((L, 128, 32), np.float32)
    pv_dt[:, :, 0:16] = inp["ssm_dt_bias"][:L][:, :, 8 * j:8 * j + 8].reshape(L, 1, 16)
    pv_dt[:, :, 16:32] = inp["ssm_a_log"][:L][:, :, 8 * j:8 * j + 8].reshape(L, 1, 16)
    fb = list(range(NBF * j, NBF * j + NBF)) + list(range(44 + NBF * j, 44 + NBF * j + NBF))
    pv_ffn = np.zeros((L, 2 * NBF, 128, 4), np.float32)
    pv_ffn[..., 0:3] = inp["ffn_conv_w"][:L].reshape(L, 3, 88, 128).transpose(0, 2, 3, 1)[:, fb]
    pv_ffn[..., 3] = inp["ffn_conv_b"][:L].reshape(L, 88, 128)[:, fb]
    up = inp["ffn_up"][:L]
    return {
        "w_inc": f(w[..., cols]), "pv_lru": pv_lru, "lru_gw": f(gw), "pv_sc": pv_sc, "pv_ssc": pv_ssc,
        "pv_ssd": pv_ssd, "pv_dt": pv_dt, "pv_ffn": pv_ffn,
        "ffn_upc": f(np.concatenate([up[..., j * DFQ:(j + 1) * DFQ], up[..., DFF + j * DFQ:DFF + (j + 1) * DFQ]], -1)),
        "cid": np.array([[j, 0, 0, 0]], np.int32),
    }


_NC_CACHE = {}
_CWH = [2048]


def run(cfg, ins_per_core):
    key = tuple(sorted(cfg.items()))
    if key not in _NC_CACHE:
        _NC_CACHE[key] = build(cfg)
    nc = _NC_CACHE[key]
    n = len(ins_per_core)
    return run_bass_kernel_spmd(nc, ins_per_core, core_ids=list(range(n)))


def kernel(**inp):
    inp = {k: np.asarray(v) for k, v in inp.items()}
    B, S, _ = inp["x"].shape
    depth = inp["w_in"].shape[0]
    TQ = S // NQ
    cfg = {"S": S, "depth": depth, "cwh": _CWH[0]}
    shared = pack_shared(inp, depth)
    shared["consts"], shared["ropeT"] = make_consts(S)
    quarters = [pack_quarter(inp, depth, j) for j in range(NQ)]
    ins = []
    for b in range(2):
        bb = min(b, B - 1)
        for j in range(NQ):
            m = dict(shared)
            m.update(quarters[j])
            m["xc"] = np.ascontiguousarray(inp["ctx"][bb].T.astype(np.float32))
            CWH = min(_CWH[0], TQ)
            xqt = inp["x"][bb][j * TQ:(j + 1) * TQ].T.astype(np.float32)
            m["xq"] = np.ascontiguousarray(xqt.reshape(NCH, 128, TQ // CWH, CWH).transpose(0, 2, 1, 3))
            m["mod"] = np.ascontiguousarray(np.stack([inp["c_ctx"], inp["c"][bb]]).astype(np.float32))
            ins.append(m)
    res = run(cfg, ins)
    out = np.zeros((B, S, D), np.float32)
    for b in range(B):
        for j in range(NQ):
            o = res.results[b * NQ + j]["out"]
            out[b, j * TQ:(j + 1) * TQ] = o.transpose(0, 2, 1, 3).reshape(D, TQ).T
    return out
```
